# Optimizing a Trainium2 kernel written in Bass

```python
import jax, jax.numpy as jnp
from jax import lax
import numpy as np

D_MODEL = 1024
BATCH = 8
SEQ = 2048
DEPTH = 4

CTX_LEN = 256
GRID_W = 64
EXPAND = 2
D_INNER = EXPAND * D_MODEL
D_FOURIER = D_INNER // 4
FOURIER_GROUP = 64
N_FOURIER_GROUPS = D_FOURIER // FOURIER_GROUP
D_SSD = D_INNER - D_FOURIER
SSD_HEAD_DIM = 64
N_SSD_HEADS = D_SSD // SSD_HEAD_DIM
N_BC_GROUPS = 4
HEADS_PER_GROUP = N_SSD_HEADS // N_BC_GROUPS
D_STATE = 128
CONV_K = 3
CONV_CH = D_SSD + 2 * N_BC_GROUPS * D_STATE
CHUNK = 128
D_PROJ = 2 * D_FOURIER + D_SSD + CONV_CH + 2 * N_SSD_HEADS
PROJ_SPLITS = (D_FOURIER, 2 * D_FOURIER, 2 * D_FOURIER + D_SSD, 2 * D_FOURIER + D_SSD + CONV_CH)
EPS = 1e-6

kernel_name = "hybrid_fourier_ssd_prefix_dit"


def rmsnorm(x, w):
    xf = x.astype(jnp.float32)
    xf = xf * lax.rsqrt(jnp.mean(xf * xf, axis=-1, keepdims=True) + EPS)
    return xf.astype(x.dtype) * w


def segsum(a):
    cum = jnp.cumsum(a, axis=-1)
    diff = cum[..., :, None] - cum[..., None, :]
    t = a.shape[-1]
    mask = jnp.tril(jnp.ones((t, t), dtype=bool))
    return jnp.where(mask, diff, -jnp.inf)


def ssd_chunked(x, dt, A, B, C, h0):
    bsz, length, g, r, p = x.shape
    n = B.shape[-1]
    nc = length // CHUNK
    xc = (x * dt[..., None]).reshape(bsz, nc, CHUNK, g, r, p)
    bc = B.reshape(bsz, nc, CHUNK, g, n)
    cc = C.reshape(bsz, nc, CHUNK, g, n)
    a = (dt * A).reshape(bsz, nc, CHUNK, g, r).transpose(0, 3, 4, 1, 2)
    a_cum = jnp.cumsum(a, axis=-1)
    lmat = jnp.exp(segsum(a))
    cb = jnp.einsum('bclgn,bcsgn->bgcls', cc, bc)
    y_diag = jnp.einsum('bgcls,bgrcls,bcsgrp->bclgrp', cb, lmat, xc)
    decay_states = jnp.exp(a_cum[..., -1:] - a_cum)
    states = jnp.einsum('bcsgn,bgrcs,bcsgrp->bcgrpn', bc, decay_states, xc)
    states = jnp.concatenate([h0[:, None], states], axis=1)
    chunk_a = jnp.pad(a_cum[..., -1], ((0, 0), (0, 0), (0, 0), (1, 0)))
    decay_chunk = jnp.exp(segsum(chunk_a))
    new_states = jnp.einsum('bgrzc,bcgrpn->bzgrpn', decay_chunk, states)
    prev_states, h_final = new_states[:, :-1], new_states[:, -1]
    y_off = jnp.einsum('bclgn,bcgrpn,bgrcl->bclgrp', cc, prev_states, jnp.exp(a_cum))
    y = (y_diag + y_off).reshape(bsz, length, g, r, p)
    return y, h_final


def depthwise_conv_grid(u, w, bias, rows, cols):
    bsz, length, ch = u.shape
    img = u.reshape(bsz, rows, cols, ch)
    out = lax.conv_general_dilated(img, w[:, :, None, :], window_strides=(1, 1), padding='SAME',
                                   dimension_numbers=('NHWC', 'HWIO', 'NHWC'), feature_group_count=ch)
    return jax.nn.silu(out.reshape(bsz, length, ch) + bias)


def fourier_mixer(u, zf, w_f, b_f):
    bsz, length, _ = u.shape
    uf = u.astype(jnp.float32).reshape(bsz, length, N_FOURIER_GROUPS, FOURIER_GROUP)
    mixed = jnp.fft.fft2(uf, axes=(1, 3), norm='ortho').real.reshape(bsz, length, D_FOURIER)
    mixed = mixed.astype(u.dtype) @ w_f + b_f
    return mixed * jax.nn.silu(zf)


def ssd_mixer(xbc, dt_raw, zs, dt_bias, a_log, d_skip, norm_w, h0_f, h0_b):
    bsz, length, _ = xbc.shape
    f32 = jnp.float32
    g, r, p, n = N_BC_GROUPS, HEADS_PER_GROUP, SSD_HEAD_DIM, D_STATE
    xs, bm, cm = jnp.split(xbc.astype(f32), [D_SSD, D_SSD + g * n], axis=-1)
    xs = xs.reshape(bsz, length, g, r, p)
    bm = bm.reshape(bsz, length, g, n)
    cm = cm.reshape(bsz, length, g, n)
    dt = jax.nn.softplus(dt_raw.astype(f32).reshape(bsz, length, 2, N_SSD_HEADS) + dt_bias.astype(f32))
    A = -jnp.exp(a_log.astype(f32)).reshape(2, g, r)
    dt_f = dt[:, :, 0].reshape(bsz, length, g, r)
    dt_b = dt[:, :, 1].reshape(bsz, length, g, r)
    flip = lambda t: jnp.flip(t, axis=1)
    y_f, h_f = ssd_chunked(xs, dt_f, A[0], bm, cm, h0_f)
    y_b, h_b = ssd_chunked(flip(xs), flip(dt_b), A[1], flip(bm), flip(cm), h0_b)
    y = y_f + flip(y_b) + d_skip.astype(f32).reshape(g, r)[:, :, None] * xs
    gated = y.reshape(bsz, length, g, r * p) * jax.nn.silu(zs.astype(f32)).reshape(bsz, length, g, r * p)
    gated = gated * lax.rsqrt(jnp.mean(gated * gated, axis=-1, keepdims=True) + EPS)
    out = gated.reshape(bsz, length, D_SSD).astype(zs.dtype) * norm_w
    return out, h_f, h_b


def token_mixer(h, rows, cols, w_in, conv_w, conv_b, dt_bias, a_log, d_skip, ssd_norm_w,
                w_fourier, b_fourier, h0_f, h0_b):
    u, zf, zs, xbc, dt_raw = jnp.split(h @ w_in, PROJ_SPLITS, axis=-1)
    four = fourier_mixer(u, zf, w_fourier, b_fourier)
    xbc = depthwise_conv_grid(xbc, conv_w, conv_b, rows, cols)
    ssd, h_f, h_b = ssd_mixer(xbc, dt_raw, zs, dt_bias, a_log, d_skip, ssd_norm_w, h0_f, h0_b)
    return jnp.concatenate([four, ssd], axis=-1), h_f, h_b


def setup_inputs(seed: int = 0) -> dict:
    key = jax.random.key(seed)
    ks = jax.random.split(key, 20)
    f32 = jnp.float32
    nrm = lambda k, shape, s: jax.random.normal(k, shape, f32) * s
    dt0 = jnp.exp(jax.random.uniform(ks[9], (DEPTH, 2, N_SSD_HEADS), f32,
                                     minval=np.log(1e-3), maxval=np.log(1e-1)))
    return {
        "x": nrm(ks[0], (BATCH, SEQ, D_MODEL), 1.0),
        "c": nrm(ks[1], (BATCH, D_MODEL), 1.0),
        "ctx": nrm(ks[2], (BATCH, CTX_LEN, D_MODEL), 1.0),
        "c_ctx": nrm(ks[3], (D_MODEL,), 1.0),
        "norm_w": 1.0 + nrm(ks[4], (DEPTH, D_MODEL), 0.05),
        "w_ada": nrm(ks[5], (DEPTH, D_MODEL, 3 * D_MODEL), 0.5 * D_MODEL ** -0.5),
        "b_ada": nrm(ks[6], (DEPTH, 3 * D_MODEL), 0.02),
        "w_in": nrm(ks[7], (DEPTH, D_MODEL, D_PROJ), D_MODEL ** -0.5),
        "conv_w": nrm(ks[8], (DEPTH, CONV_K, CONV_K, CONV_CH), 1.0 / CONV_K),
        "conv_b": nrm(ks[10], (DEPTH, CONV_CH), 0.02),
        "dt_bias": dt0 + jnp.log(-jnp.expm1(-dt0)),
        "a_log": jnp.log(jax.random.uniform(ks[11], (DEPTH, 2, N_SSD_HEADS), f32, minval=1.0, maxval=16.0)),
        "d_skip": 1.0 + nrm(ks[12], (DEPTH, N_SSD_HEADS), 0.1),
        "ssd_norm_w": 1.0 + nrm(ks[13], (DEPTH, D_SSD), 0.05),
        "w_fourier": nrm(ks[14], (DEPTH, D_FOURIER, D_FOURIER), D_FOURIER ** -0.5),
        "b_fourier": nrm(ks[15], (DEPTH, D_FOURIER), 0.02),
        "w_out": nrm(ks[16], (DEPTH, D_INNER, D_MODEL), D_INNER ** -0.5),
        "final_norm_w": 1.0 + nrm(ks[17], (D_MODEL,), 0.05),
    }


def reference(x, c, ctx, c_ctx, norm_w, w_ada, b_ada, w_in, conv_w, conv_b, dt_bias, a_log, d_skip,
              ssd_norm_w, w_fourier, b_fourier, w_out, final_norm_w):
    bsz, seq_len, _ = x.shape
    ctx_len = ctx.shape[1]
    rows = seq_len // GRID_W
    zeros_state = jnp.zeros((bsz, N_BC_GROUPS, HEADS_PER_GROUP, SSD_HEAD_DIM, D_STATE), jnp.float32)
    silu_c = jax.nn.silu(c)
    silu_cc = jax.nn.silu(c_ctx)
    for i in range(DEPTH):
        mod = silu_c @ w_ada[i] + b_ada[i]
        shift, scale, gate = jnp.split(mod, 3, axis=-1)
        mod_c = silu_cc @ w_ada[i] + b_ada[i]
        shift_c, scale_c, gate_c = jnp.split(mod_c, 3, axis=-1)
        layer = (w_in[i], conv_w[i], conv_b[i], dt_bias[i], a_log[i], d_skip[i], ssd_norm_w[i],
                 w_fourier[i], b_fourier[i])
        hc = rmsnorm(ctx, norm_w[i]) * (1.0 + scale_c) + shift_c
        ctx_mix, h_f, h_b = token_mixer(hc, 1, ctx_len, *layer, zeros_state, zeros_state)
        hx = rmsnorm(x, norm_w[i]) * (1.0 + scale[:, None]) + shift[:, None]
        x_mix, _, _ = token_mixer(hx, rows, GRID_W, *layer, h_f, h_b)
        x = x + gate[:, None] * (x_mix @ w_out[i])
        if i < DEPTH - 1:
            ctx = ctx + gate_c * (ctx_mix @ w_out[i])
    return rmsnorm(x, final_norm_w)
```

```python
import numpy as np
import concourse.bass as bass
import concourse.mybir as mybir
from concourse.bass_utils import run_bass_kernel_spmd

F32 = mybir.dt.float32
BF16 = mybir.dt.bfloat16
AF = mybir.ActivationFunctionType
ALU = mybir.AluOpType

ENGS = ["pe", "act", "dve", "pool", "sp"]
NDS = 8
SAME_ENG_WAIT = False


class Prog:
    def __init__(self, nc):
        self.nc = nc
        self.ops = []

    def add(self, eng, fn, reads=(), writes=(), dma=False):
        self.ops.append((eng, fn, tuple(reads), tuple(writes), dma))

    def barrier(self):
        self.ops.append(("bar", None, (), (), False))

    def mm(self, out, lhsT, rhs, start, stop, reads, writes, **kw):
        self.add("pe", lambda e: e.matmul(out, lhsT, rhs, start=start, stop=stop, **kw), reads, writes)

    def tr(self, out, in_, ident, reads, writes):
        self.add("pe", lambda e: e.transpose(out, in_, ident), reads, writes)

    def act(self, out, in_, func, reads, writes, **kw):
        self.add("act", lambda e: e.activation(out, in_, func, **kw), reads, writes)

    def tt(self, eng, out, in0, in1, op, reads, writes):
        self.add(eng, lambda e: e.tensor_tensor(out, in0, in1, op), reads, writes)

    def ts(self, eng, out, in0, s1, s2, op0, op1, reads, writes):
        if op1 is None:
            self.add(eng, lambda e: e.tensor_scalar(out, in0, s1, None, op0), reads, writes)
        else:
            self.add(eng, lambda e: e.tensor_scalar(out, in0, s1, s2, op0, op1), reads, writes)

    def stt(self, out, in0, scalar, in1, op0, op1, reads, writes):
        self.add("dve", lambda e: e.scalar_tensor_tensor(out, in0, scalar, in1, op0, op1), reads, writes)

    def copy(self, eng, out, in_, reads, writes):
        if eng == "act":
            self.add("act", lambda e: e.copy(out, in_), reads, writes)
        else:
            self.add(eng, lambda e: e.tensor_copy(out, in_), reads, writes)

    def dma(self, eng, out, in_, reads, writes, **kw):
        self.add(eng, lambda e: e.dma_start(out, in_, **kw), reads, writes, dma=True)

    def emit(self):
        nc = self.nc
        ops = self.ops
        n = len(ops)
        cnt = {e: 0 for e in ENGS}
        sig = [None] * n
        slot = {e: 0 for e in ENGS}
        dcount = {}
        prevuse = [None] * n
        last_w = {}
        readers = {}
        deps = [None] * n
        for i, (eng, fn, r, w, dma) in enumerate(ops):
            if eng == "bar":
                deps[i] = set()
                continue
            d = set()
            for x in r:
                if x in last_w:
                    d.add(last_w[x])
            for x in w:
                if x in last_w:
                    d.add(last_w[x])
                rd = readers.get(x)
                if rd:
                    d.update(rd.values())
            d.discard(i)
            deps[i] = d
            for x in r:
                readers.setdefault(x, {})[("d", i) if dma else eng] = i
            for x in w:
                last_w[x] = i
                readers[x] = {}
        needed = set()
        for i in range(n):
            needed |= deps[i]
        lastc = {}
        for i, (eng, fn, r, w, dma) in enumerate(ops):
            if eng == "bar":
                needed.update(lastc.values())
            elif not dma:
                lastc[eng] = i
        needed.update(lastc.values())
        snaps = {}
        for i, (eng, fn, r, w, dma) in enumerate(ops):
            if eng == "bar":
                snaps[i] = (dict(cnt), dict(dcount))
                continue
            if dma:
                k = slot[eng] % NDS
                slot[eng] += 1
                key = (eng, k)
                prev = dcount.get(key, 0)
                if prev > 0:
                    prevuse[i] = (key, prev)
                dcount[key] = prev + 16
                sig[i] = ("d", key, prev + 16)
            elif i in needed:
                cnt[eng] += 1
                sig[i] = ("c", eng, cnt[eng])
        sems = {}
        for e in ENGS:
            if cnt[e] > 0:
                sems[("c", e)] = nc.alloc_semaphore(f"c_{e}")
        for key in dcount:
            sems[("d", key)] = nc.alloc_semaphore(f"d_{key[0]}{key[1]}")
        self.n_ops = n
        by_eng = {e: [i for i in range(n) if ops[i][0] == e or ops[i][0] == "bar"] for e in ENGS}

        def run(eng, eobj):
            waited = {}
            for i in by_eng[eng]:
                _, fn, r, w, dma = ops[i]
                if ops[i][0] == "bar":
                    c_snap, d_snap = snaps[i]
                    for f_, v_ in c_snap.items():
                        if v_ > 0 and f_ != eng and waited.get(("c", f_), 0) < v_:
                            eobj.wait_ge(sems[("c", f_)], v_)
                            waited[("c", f_)] = v_
                    for k_, v_ in d_snap.items():
                        if v_ > 0 and waited.get(("d", k_), 0) < v_:
                            eobj.wait_ge(sems[("d", k_)], v_)
                            waited[("d", k_)] = v_
                    continue
                need = {}
                for d in deps[i]:
                    s = sig[d]
                    if s[0] == "c":
                        if s[1] == eng and not dma and (eng == "pe" or not SAME_ENG_WAIT):
                            continue
                        key = ("c", s[1])
                    else:
                        key = ("d", s[1])
                    if need.get(key, 0) < s[2]:
                        need[key] = s[2]
                if prevuse[i] is not None:
                    key = ("d", prevuse[i][0])
                    if need.get(key, 0) < prevuse[i][1]:
                        need[key] = prevuse[i][1]
                for key, val in need.items():
                    if waited.get(key, 0) < val:
                        eobj.wait_ge(sems[key], val)
                        waited[key] = val
                ins = fn(eobj)
                s = sig[i]
                if s is None:
                    pass
                elif s[0] == "c":
                    ins.then_inc(sems[("c", eng)], 1)
                else:
                    ins.then_inc(sems[("d", s[1])], 16)
            for key, val in dcount.items():
                if key[0] == eng and waited.get(("d", key), 0) < val:
                    eobj.wait_ge(sems[("d", key)], val)

        with nc.Block() as block:

            @block.tensor
            def _(e):
                run("pe", e)

            @block.scalar
            def _(e):
                run("act", e)

            @block.vector
            def _(e):
                run("dve", e)

            @block.gpsimd
            def _(e):
                run("pool", e)

            @block.sync
            def _(e):
                run("sp", e)


import ml_dtypes

D = 1024; DEPTH = 4; SEQ = 2048; CTXL = 256
DPROJ = 5168; NH = 24; EPS = 1e-6
PL = 274
NV = DEPTH * PL + 24
BIG = 3.0e38


class SB:
    def __init__(self, nc):
        self.nc = nc
        self.top = ((nc.sbuf_base + 63) // 64) * 64
        self.lim = nc.sbuf_top
        self.k = 0
        self.peak = 0

    def alloc(self, shape, dt):
        nbytes = int(np.prod(shape[1:])) * (4 if dt == F32 else 2)
        off = self.top
        self.top = ((off + nbytes + 63) // 64) * 64
        self.peak = max(self.peak, self.top)
        assert self.top <= self.lim, f"SBUF overflow {self.top} > {self.lim}"
        self.k += 1
        return self.nc.alloc_sbuf_tensor_at(f"t{self.k}", list(shape), dt, offset=off).ap()


class Rot:
    def __init__(self, sb, name, shape, dt, n):
        self.b = [(sb.alloc(shape, dt), f"{name}{i}") for i in range(n)]
        self.i = 0

    def next(self):
        r = self.b[self.i % len(self.b)]
        self.i += 1
        return r


def vo(l, what):
    base = l * PL
    return {"nw": (base, 8), "bada": (base + 8, 24), "convb": (base + 32, 20), "cw": (base + 52, 180),
            "dtb": (base + 232, 1), "alog": (base + 233, 1), "dskip": (base + 234, 24),
            "snw": (base + 258, 12), "bf": (base + 270, 4)}[what]


def build(depth=DEPTH, dbg=False):
    nc = bass.Bass("TRN2", target_bir_lowering=False)
    din = lambda name, shape, dt=F32: nc.dram_tensor(name, list(shape), dt, kind="ExternalInput").ap()
    x_in = din("x", [SEQ, D]); ctx_in = din("ctx", [CTXL, D])
    vecs_in = din("vecs", [128, NV])
    w_ada = din("w_ada", [DEPTH, D, 3 * D]); w_in = din("w_in", [DEPTH, D, DPROJ])
    w_f = din("w_fourier", [DEPTH, 512, 512]); w_out = din("w_out", [DEPTH, 2048, D])
    conv_b = din("conv_b", [DEPTH, 2560])
    NCB = 512 + 48 * 128 + 256
    cbf_in = din("cbf", [128, NCB], BF16); cf32_in = din("cf32", [128, 4 * 128 + 4])
    dftin = {2048: (din("cl2048", [2048, 2048], BF16), din("sl2048", [2048, 2048], BF16)),
             256: (din("cl256", [256, 256], BF16), din("sl256", [256, 256], BF16))}
    out = nc.dram_tensor("out", [SEQ, D], F32, kind="ExternalOutput").ap()
    dscr = lambda name, shape, dt: (nc.dram_tensor(name, list(shape), dt, kind="ExternalOutput").ap() if dbg
                                    else nc.dram_tensor(name, list(shape), dt).ap())
    dbgout = {}

    P = Prog(nc)
    sb = SB(nc)
    PS = [nc.alloc_psum_tensor(f"ps{k}", [128, 512], F32).ap() for k in range(8)]
    bk = [0]

    bankpool = [list(range(6))]

    def bank():
        k = bankpool[0][bk[0] % len(bankpool[0])]
        bk[0] += 1
        return PS[k], f"ps{k}"

    yk = [0]

    def ybank_():
        k = 6 + yk[0] % 2
        yk[0] += 1
        return PS[k], f"ps{k}"

    cbf = sb.alloc([128, NCB], BF16)
    cf32 = sb.alloc([128, 516], F32)
    vecs = sb.alloc([128, NV], F32)
    ident = cbf[:, 0:128]; ones = cbf[:, 128:256]; bdcs = cbf[:, 256:512]
    esel = cbf[:, 512:512 + 48 * 128].rearrange("p (q s) -> p q s", q=48)
    negm = [cbf[:, 512 + 48 * 128:512 + 48 * 128 + 128], cbf[:, 512 + 48 * 128 + 128:NCB]]
    ident32 = cf32[:, 0:128]; maskd = [cf32[:, 128:256], cf32[:, 256:384]]; ones32 = cf32[:, 384:512]
    isf = cf32[:, 512:513]; isb = cf32[:, 513:514]; sgn = cf32[:, 514:515]
    resetm = sb.alloc([128, SEQ], BF16)
    sc2 = sb.alloc([128, 8, 2], BF16)
    modT2 = [sb.alloc([128, 24, 2], F32) for _ in range(2)]; Amod2 = [sb.alloc([128, 8, 2], F32) for _ in range(2)]
    Acol = sb.alloc([128, 1], F32)
    wrot = Rot(sb, "wb", [128, 8, 512], BF16, 3)
    TOT = SEQ + CTXL
    hT = sb.alloc([128, 8, TOT], BF16)
    uT = sb.alloc([128, 4, TOT], BF16)
    zrow = sb.alloc([128, TOT], F32)
    Sst = [sb.alloc([128, 1536], F32) for _ in range(2)]
    Sbf = [sb.alloc([128, 1536], BF16) for _ in range(2)]
    DI = sb.alloc([128, 24, 128], BF16)
    arena0 = sb.top

    P.dma("sp", cbf, cbf_in, [], ["cbf"])
    P.dma("sp", cf32, cf32_in, [], ["cf32"])
    P.dma("sp", vecs, vecs_in, [], ["vecs"])
    P.add("pool", lambda e: e.memset(resetm, 1.0), [], ["resetm"])
    P.add("pool", lambda e: e.memset(resetm.rearrange("p (c t) -> p c t", t=128)[:, :, 0:1], 0.0), ["resetm"], ["resetm"])
    P.add("pool", lambda e: e.memset(zrow, 1.0), [], ["zrow"])
    P.act(sc2[:, :, 0], vecs[:, NV - 16:NV - 8], AF.Silu, ["vecs"], ["sc2"])
    P.act(sc2[:, :, 1], vecs[:, NV - 8:NV], AF.Silu, ["vecs"], ["sc2"])

    passes = {}
    for name, L in (("c", CTXL), ("x", SEQ)):
        NT = min(512, L)
        passes[name] = dict(
            name=name, L=L, NT=NT, nblk=L // NT, nch=L // 128, pi=(1 if name == "c" else 0),
            res=dscr(f"res_{name}", [8, 128, L], F32), gates=dscr(f"gates_{name}", [16, 128, L], BF16),
            xs=dscr(f"xs_{name}", [L, 1536], BF16), btok=dscr(f"btok_{name}", [L, 512], BF16),
            bt=dscr(f"bt_{name}", [4, 128, L], BF16), ct=dscr(f"ct_{name}", [4, 128, L], BF16),
            mix=dscr(f"mix_{name}", [16, 128, L], BF16), gst=dscr(f"gst_{name}", [L // 128, 128, 1536], BF16),
            off=(SEQ if name == "c" else 0), R=(1 if name == "c" else 32), W=(256 if name == "c" else 64), dft=dftin[L])
    flip = [0]

    def alt():
        flip[0] ^= 1
        return "act" if flip[0] else "dve"

    def load_resid(ps, src):
        sb.top = arena0
        xl = Rot(sb, "xl", [128, D], F32, 2); xs_ = Rot(sb, "xst", [128, 4, 128], F32, 2)
        for tt in range(ps["nch"]):
            t, tr = xl.next()
            P.dma("sp", t, src[tt * 128:(tt + 1) * 128, :], [], [tr])
            for half in range(2):
                pb, pr = bank()
                for k in range(4):
                    c = half * 4 + k
                    P.tr(pb[:, k * 128:(k + 1) * 128], t[:, c * 128:(c + 1) * 128], ident32, [tr, "cf32"], [pr])
                st, sr = xs_.next()
                P.copy(alt(), st, pb.rearrange("p (k t) -> p k t", k=4), [pr], [sr])
                dst = ps["res"][half * 4:half * 4 + 4, :, tt * 128:(tt + 1) * 128].rearrange("c p t -> p c t")
                P.dma("pool", dst, st, [sr], [f"res{ps['name']}{half * 4 + k}" for k in range(4)])
        P.barrier()

    def adaln_steps(l):
        modT = modT2[l % 2]; Amod = Amod2[l % 2]; mr = f"modT{l % 2}"; ar = f"Amod{l % 2}"
        steps = []

        def grp(g):
            wb, wr = wrot.next()
            P.dma("pool", wb, w_ada[l].rearrange("(c p) j -> p c j", p=128)[:, :, g * 512:(g + 1) * 512], [], [wr])
            pb, pr = bank()
            for jj in range(4):
                for c in range(8):
                    P.mm(pb[:, jj * 2:jj * 2 + 2], wb[:, c, jj * 128:(jj + 1) * 128], sc2[:, c, :], c == 0, c == 7, [wr, "sc2"], [pr])
            o, n_ = vo(l, "bada")
            P.tt("dve", modT[:, g * 4:(g + 1) * 4, :], pb[:, 0:8].rearrange("p (j t) -> p j t", t=2),
                 vecs[:, o + g * 4:o + g * 4 + 4].unsqueeze(2).to_broadcast([128, 4, 2]), ALU.add, [pr, "vecs"], [mr])

        def fin():
            o, n_ = vo(l, "nw")
            P.stt(Amod, modT[:, 8:16, :], 1.0, vecs[:, o:o + 8].unsqueeze(2).to_broadcast([128, 8, 2]), ALU.add, ALU.mult, [mr, "vecs"], [ar])

        for g in range(6):
            steps.append(lambda g=g: grp(g))
        steps.append(fin)
        return steps

    def adaln_small(l):
        o, n_ = vo(l, "alog")
        P.act(Acol, vecs[:, o:o + 1], AF.Exp, ["vecs"], ["Acol"])
        P.ts("dve", Acol, Acol, -1.0, None, ALU.mult, None, ["Acol"], ["Acol"])
        o, n_ = vo(l, "dskip")
        P.tt("pool", DI, ident.unsqueeze(1).to_broadcast([128, 24, 128]), vecs[:, o:o + 24].unsqueeze(2).to_broadcast([128, 24, 128]),
             ALU.mult, ["cbf", "vecs"], ["DI"])

    def rms_block(ps, n, xb, xr, sqrot, r1, rstd, nfeat_chunks, denom):
        NT = ps["NT"]
        pb, pr = bank()
        for c in range(nfeat_chunks):
            sq, sr = sqrot.next()
            P.act(sq[:, :NT], xb[:, c, :NT], AF.Square, [xr], [sr])
            P.mm(pb[:, :NT], ones, sq[:, :NT], c == 0, c == nfeat_chunks - 1, [sr, "cbf"], [pr])
        P.act(r1[:, :NT], pb[:, :NT], AF.Sqrt, [pr], ["r1"], bias=EPS, scale=1.0 / denom)
        P.add("dve", lambda e: e.reciprocal(rstd[:, :NT], r1[:, :NT]), ["r1"], ["rstd"])

    winpre = {}

    def phase_norm(l):
        sb.top = arena0
        for g in range(3):
            wb, wr = wrot.next()
            P.dma("pool", wb, w_in[l].rearrange("(c p) j -> p c j", p=128)[:, :, g * 512:(g + 1) * 512], [], [wr])
            winpre[g] = (wb, wr)
        xrot = Rot(sb, "xb", [128, 8, 512], F32, 2); sqrot = Rot(sb, "sq", [128, 512], BF16, 2)
        r1 = sb.alloc([128, 512], F32); rstd = sb.alloc([128, 512], F32); tmpr = Rot(sb, "ntmp", [128, 512], F32, 2)
        for ps in (pc, px):
            NT = ps["NT"]; pi = ps["pi"]
            for n in range(ps["nblk"]):
                sl = slice(n * NT, (n + 1) * NT)
                hsl = slice(ps["off"] + n * NT, ps["off"] + (n + 1) * NT)
                xb, xr = xrot.next()
                P.dma("sp", xb[:, :, :NT], ps["res"][:, :, sl].rearrange("c p t -> p c t"), [f"res{ps['name']}{c}" for c in range(8)], [xr])
                rms_block(ps, n, xb, xr, sqrot, r1, rstd, 8, float(D))
                for c in range(8):
                    tm, tmr = tmpr.next()
                    P.stt(tm[:, :NT], xb[:, c, :NT], Amod2[l % 2][:, c, pi:pi + 1], rstd[:, :NT], ALU.mult, ALU.mult, [xr, f"Amod{l % 2}", "rstd"], [tmr])
                    P.act(hT[:, c, hsl], tm[:, :NT], AF.Identity, [tmr, f"modT{l % 2}"], ["hT"], bias=modT2[l % 2][:, c, pi:pi + 1], scale=1.0)
        P.barrier()

    def conv_block(l, ps, cb, xsh, xpr, dg, dgr, cstage, tstage):
        L = ps["L"]; NT = ps["NT"]; W = ps["W"]; nm = ps["name"]
        taps = [(dr, dc) for dr in ((-1, 0, 1) if ps["R"] > 1 else (0,)) for dc in (-1, 0, 1)]
        win = lambda dr, dc, t0, nt: xsh[dc + 1][:, (1 + dr) * W + t0:(1 + dr) * W + t0 + nt]
        tapi = lambda dr, dc: (dr + 1) * 3 + (dc + 1)
        o2, n_ = vo(l, "convb")

        def tstage_(n, cs, csr):
            dstt = ps["xs"] if cb < 12 else ps["btok"]
            ch0 = cb * 128 if cb < 12 else (cb - 12) * 128
            k4 = NT // 128
            pt, ptr = bank()
            ptb = pt.bitcast(BF16)
            for k in range(k4):
                P.tr(ptb[:, k * 128:(k + 1) * 128], cs[:, k * 128:(k + 1) * 128], ident, [csr, "cbf"], [ptr])
            ts_, tsr = tstage.next()
            P.copy("dve", ts_[:, :k4, :], ptb[:, :k4 * 128].rearrange("p (k c) -> p k c", c=128), [ptr], [tsr])
            d_ = dstt[n * NT:(n + 1) * NT, ch0:ch0 + 128].rearrange("(k p) c -> p k c", p=128)
            P.dma("sp", d_, ts_[:, :k4, :], [tsr], [f"{'xs' if cb < 12 else 'btok'}{nm}_{tt_}" for tt_ in range(n * k4, (n + 1) * k4)])

        prev = None
        for n in range(ps["nblk"]):
            pb, pr = bank()
            for ti, (dr, dc) in enumerate(taps):
                P.mm(pb[:, :NT], dg[:, tapi(dr, dc), :], win(dr, dc, n * NT, NT), ti == 0, ti == len(taps) - 1, [dgr, xpr], [pr])
            cs, csr = cstage.next()
            P.act(cs[:, :NT], pb[:, :NT], AF.Silu, [pr, "vecs"], [csr], bias=vecs[:, o2 + cb:o2 + cb + 1], scale=1.0)
            if cb >= 12:
                dst = ps["bt"] if cb < 16 else ps["ct"]
                P.dma("sp", dst[cb % 4][:, n * NT:(n + 1) * NT], cs[:, :NT], [csr], [f"{'bt' if cb < 16 else 'ct'}{nm}{cb % 4}_{n}"])
            if cb < 16:
                if prev is not None:
                    tstage_(*prev)
                prev = (n, cs, csr)
        if prev is not None:
            tstage_(*prev)

    def phase_inproj(l):
        sb.top = arena0
        gstage = Rot(sb, "gs", [128, 512], BF16, 3); dgrot = Rot(sb, "dg", [128, 9, 128], BF16, 2)
        cstage = Rot(sb, "cs", [128, 512], BF16, 2); tstage = Rot(sb, "tst", [128, 4, 128], BF16, 3)
        xsh = {}
        for ps in (pc, px):
            xsh[ps["name"]] = [[(sb.alloc([128, (ps["R"] + 2) * ps["W"]], BF16), f"xsh{ps['name']}{b_}") for _ in range(3)] for b_ in range(2)]
            for b_ in range(2):
                for (t, r) in xsh[ps["name"]][b_]:
                    P.add("pool", lambda e, t=t: e.memset(t, 0.0), [], [r])
        wv = w_in[l].rearrange("(c p) j -> p c j", p=128)
        P.add("pool", lambda e: e.memset(zrow, 1.0), [], ["zrow"])
        blocks = [(ps, n) for ps in (pc, px) for n in range(ps["nblk"])]
        xi = 0
        pendc = []
        for g in range(11):
            width = 512 if g < 10 else 48
            if g in winpre:
                wb, wr = winpre.pop(g)
            else:
                wb, wr = wrot.next()
                P.dma("pool", wb[:, :, :width], wv[:, :, g * 512:g * 512 + width], [], [wr])
            if g == 10:
                o, n_ = vo(l, "dtb")
                for ps, n in blocks:
                    NT = ps["NT"]
                    hsl = slice(ps["off"] + n * NT, ps["off"] + (n + 1) * NT)
                    pb, pr = bank()
                    for c in range(8):
                        P.mm(pb[0:48, :NT], wb[:, c, 0:48], hT[:, c, hsl], c == 0, c == 7, [wr, "hT"], [pr])
                    for c in range(8):
                        P.mm(pb[64:112, :NT], wb[:, c, 0:48], hT[:, c, hsl], c == 0, c == 7, [wr, "hT"], [pr], tile_position=(0, 64))
                    P.act(zrow[0:48, hsl], pb[0:48, :NT], AF.Exp, [pr, "vecs"], ["zrow"], bias=vecs[0:48, o:o + 1], scale=1.0)
                    P.act(zrow[64:112, hsl], pb[64:112, :NT], AF.Exp, [pr, "vecs"], ["zrow"], bias=vecs[64:112, o:o + 1], scale=1.0)
                    if pendc:
                        pendc.pop(0)()
                continue
            for jj in range(4):
                jb = g * 4 + jj
                if jb >= 20:
                    xi += 1
                for ps, n in blocks:
                    NT = ps["NT"]; nm = ps["name"]; W = ps["W"]
                    sl = slice(n * NT, (n + 1) * NT)
                    hsl = slice(ps["off"] + n * NT, ps["off"] + (n + 1) * NT)
                    pb, pr = bank()
                    for c in range(8):
                        P.mm(pb[:, :NT], wb[:, c, jj * 128:(jj + 1) * 128], hT[:, c, hsl], c == 0, c == 7, [wr, "hT"], [pr])
                    if jb < 4:
                        P.copy(alt(), uT[:, jb, hsl], pb[:, :NT], [pr], [f"uT{jb}"])
                    elif jb < 20:
                        gs, gr = gstage.next()
                        P.act(gs[:, :NT], pb[:, :NT], AF.Silu, [pr], [gr])
                        P.dma("sp", ps["gates"][jb - 4][:, sl], gs[:, :NT], [gr], [f"gate{nm}{jb - 4}_{n}"])
                    else:
                        xs3 = xsh[nm][xi % 2]
                        nr = NT // W
                        r0 = 1 + n * nr
                        src = pb[:, :NT].rearrange("p (r w) -> p r w", w=W)
                        v3 = lambda t: t.rearrange("p (r w) -> p r w", w=W)[:, r0:r0 + nr, :]
                        P.copy(alt(), v3(xs3[1][0]), src, [pr], [xs3[1][1]])
                        P.copy("pool", v3(xs3[0][0])[:, :, 1:W], v3(xs3[1][0])[:, :, 0:W - 1], [xs3[1][1]], [xs3[0][1]])
                        P.copy("pool", v3(xs3[2][0])[:, :, 0:W - 1], v3(xs3[1][0])[:, :, 1:W], [xs3[1][1]], [xs3[2][1]])
                if pendc:
                    pendc.pop(0)()
                if jb >= 20:
                    dg, dgr = dgrot.next()
                    o, n_ = vo(l, "cw")
                    cwv = vecs[:, o + (jb - 20):o + 180:20]
                    P.tt("pool", dg, ident.unsqueeze(1).to_broadcast([128, 9, 128]), cwv.unsqueeze(2).to_broadcast([128, 9, 128]),
                         ALU.mult, ["cbf", "vecs"], [dgr])

                    def do_conv(jb=jb, xi_=xi, dg=dg, dgr=dgr):
                        for ps in (pc, px):
                            xs3 = xsh[ps["name"]][xi_ % 2]
                            conv_block(l, ps, jb - 20, [t for (t, r) in xs3], xs3[0][1], dg, dgr, cstage, tstage)
                    pendc.append(do_conv)
        while pendc:
            pendc.pop(0)()
        P.barrier()

    def fft_bufs(l):
        F = dict(dfr=Rot(sb, "dft", [128, 2, 4, 512], BF16, 2), mixedT=sb.alloc([128, 4, 512], BF16), wf=sb.alloc([128, 4, 512], BF16),
                 glr=Rot(sb, "gl", [128, 512], BF16, 2), ftmp=Rot(sb, "ftmp", [128, 512], F32, 2), mst=Rot(sb, "mst", [128, 512], BF16, 2))
        P.dma("pool", F["wf"], w_f[l].rearrange("(c p) j -> p c j", p=128), [], ["wf"])
        return F

    def phase_fft(l, ps, F, tick=None, fixed=False):
        L = ps["L"]; NT = ps["NT"]; nch = ps["nch"]
        V = hT.rearrange("p c t -> p (c t)")[:, :nch * 1024].rearrange("p (t j s c) -> p t j s c", j=4, s=2, c=128)
        TG = min(4, nch)
        dfr = F["dfr"]; mixedT = F["mixedT"]; wf = F["wf"]; glr = F["glr"]; ftmp = F["ftmp"]; mst = F["mst"]
        tk = tick if tick is not None else (lambda: None)
        for tt in range(nch):
            for hf in range(2):
                pb, pr = bank()
                for jj in range(2):
                    j = hf * 2 + jj
                    P.mm(pb[:, jj * 256:(jj + 1) * 256], uT[:, j, ps["off"] + tt * 128:ps["off"] + (tt + 1) * 128], bdcs, True, True, [f"uT{j}", "cbf"], [pr])
                P.copy(alt(), V[:, tt, hf * 2:hf * 2 + 2, :, :], pb.rearrange("p (j s c) -> p j s c", j=2, s=2), [pr], ["V"])
            if tt % 2 == 1:
                tk()
        CLd, SLd = ps["dft"]
        o, n_ = vo(l, "bf")
        for kb in range(ps["nblk"]):
            sl = slice(kb * NT, (kb + 1) * NT)
            pbs = [(PS[4 + j], f"ps{4 + j}") for j in range(4)] if fixed else [bank() for _ in range(4)]
            for tg in range(nch // TG):
                df, dr_ = dfr.next()
                for s_, M_ in enumerate((CLd, SLd)):
                    P.dma("sp", df[:, s_, :TG, :NT], M_[tg * TG * 128:(tg + 1) * TG * 128, sl].rearrange("(a p) k -> p a k", p=128), [], [dr_])
                for t_ in range(TG):
                    tt = tg * TG + t_
                    for s_ in range(2):
                        for j in range(4):
                            P.mm(pbs[j][0][:, :NT], V[:, tt, j, s_, :], df[:, s_, t_, :NT], tt == 0 and s_ == 0, tt == nch - 1 and s_ == 1, ["V", dr_], [pbs[j][1]])
                tk()
            for j in range(4):
                P.copy(alt(), mixedT[:, j, :NT], pbs[j][0][:, :NT], [pbs[j][1]], ["mixedT"])
            for jo in range(4):
                pb, pr = bank()
                for ji in range(4):
                    P.mm(pb[:, :NT], wf[:, ji, jo * 128:(jo + 1) * 128], mixedT[:, ji, :NT], ji == 0, ji == 3, ["wf", "mixedT"], [pr])
                gl, glr_ = glr.next()
                P.dma("sp", gl[:, :NT], ps["gates"][jo][:, sl], [f"gate{ps['name']}{jo}_{kb}"], [glr_])
                ft, ftr = ftmp.next()
                P.act(ft[:, :NT], pb[:, :NT], AF.Identity, [pr, "vecs"], [ftr], bias=vecs[:, o + jo:o + jo + 1], scale=1.0)
                ms, msr = mst.next()
                P.tt("dve", ms[:, :NT], ft[:, :NT], gl[:, :NT], ALU.mult, [ftr, glr_], [msr])
                P.dma("pool", ps["mix"][jo][:, sl], ms[:, :NT], [msr], [f"mix{ps['name']}{jo}_{kb}"])

    def ssd_common(ps, reset=True):
        if reset:
            sb.top = arena0
        L = ps["L"]; nch = ps["nch"]
        return dict(RL=sb.alloc([128, L], BF16), RS=sb.alloc([128, L], BF16), qend=sb.alloc([128, nch], F32), qtmp=sb.alloc([128, nch], F32),
                    wtok=sb.alloc([128, nch, 48], F32), dtb=sb.alloc([128, nch, 48], F32), Dg=sb.alloc([48, nch, 48], F32))

    def ssd_prep(l, ps, K, roff=0):
        L = ps["L"]; nch = ps["nch"]
        RL = K["RL"]; RS = K["RS"]; qend = K["qend"]; qtmp = K["qtmp"]; wtok = K["wtok"]; dtb_ = K["dtb"]; Dg = K["Dg"]
        rows = hT.rearrange("p c t -> p (c t)").bitcast(F32)
        r_dt = rows[:, roff:roff + L]; r_ln = rows[:, roff + L:roff + 2 * L]; r_a = rows[:, roff + 2 * L:roff + 3 * L]; r_q = rows[:, roff + 3 * L:roff + 4 * L]
        z = zrow[:, ps["off"]:ps["off"] + L]
        c3 = lambda a: a.rearrange("p (c t) -> p c t", t=128)
        R = [f"rows{ps['name']}"]
        P.act(r_dt, z, AF.Ln, ["zrow"], R, bias=1.0, scale=1.0)
        P.act(r_ln, r_dt, AF.Ln, R, R)
        P.act(r_a, r_dt, AF.Identity, R + ["Acol"], R, scale=Acol[:, 0:1])
        P.add("dve", lambda e: e.tensor_tensor_scan(r_q, resetm[:, :L], r_a, 0.0, ALU.mult, ALU.add), R + ["resetm"], R)
        P.tt("dve", c3(r_dt), c3(r_a), c3(r_q)[:, :, 127:128].to_broadcast([128, nch, 128]), ALU.add, R, R)
        P.ts("dve", r_dt, r_dt, isb, None, ALU.mult, None, R + ["cf32"], R)
        P.stt(r_q, r_q, sgn, r_dt, ALU.mult, ALU.add, R + ["cf32"], R)
        P.ts("dve", qtmp, c3(r_q)[:, :, 0], isb, None, ALU.mult, None, R + ["cf32"], ["qtmp"])
        P.stt(qend, c3(r_q)[:, :, 127], isf, qtmp, ALU.mult, ALU.add, R + ["cf32", "qtmp"], ["qend"])
        P.copy("act", RL, r_q, R, ["RL"])
        P.tt("dve", r_dt, r_q, RL, ALU.subtract, R + ["RL"], R)
        P.copy("act", RL[64:112, :], r_dt[64:112, :], R, ["RL"])
        P.tt("dve", r_a, r_ln, r_q, ALU.subtract, R, R)
        P.copy("act", RS, r_a, R, ["RS"])
        P.tt("dve", r_dt, r_a, RS, ALU.subtract, R + ["RS"], R)
        P.copy("act", RS[64:112, :], r_dt[64:112, :], R, ["RS"])
        P.tt("dve", c3(r_dt), c3(r_a), qend.unsqueeze(2).to_broadcast([128, nch, 128]), ALU.add, R + ["qend"], R)
        P.act(r_dt, r_dt, AF.Exp, R, R)
        for c in range(nch):
            pb, pr = bank()
            P.tr(pb[:, 0:48], r_dt[0:48, c * 128:(c + 1) * 128], ident32[0:48, 0:48], R + ["cf32"], [pr])
            P.copy(alt(), wtok[:, c, :], pb[:, 0:48], [pr], ["wtok"])
        P.tt("dve", Dg, ident32[0:48, 0:48].unsqueeze(1).to_broadcast([48, nch, 48]), qend[0:48, :].unsqueeze(2).to_broadcast([48, nch, 48]),
             ALU.mult, ["cf32", "qend"], ["Dg"])
        for c0 in range(0, nch, 8):
            c1 = min(nch, c0 + 8)
            pb, pr = bank()
            P.mm(pb[:, :(c1 - c0) * 48], ones32[0:48, :], Dg[:, c0:c1, :].rearrange("p c q -> p (c q)"), True, True, ["cf32", "Dg"], [pr])
            P.act(dtb_[:, c0:c1, :].rearrange("p c q -> p (c q)"), pb[:, :(c1 - c0) * 48], AF.Exp, [pr], ["dtotbc"])

    def ssd_ctx(ps, K, fwd, scratch=None, gc_n=2, el_n=4, pieces=None):
        C = dict(K)
        C["xsr"] = Rot(sb, "xsc", [128, 1536], BF16, 2); C["btr"] = Rot(sb, "btc", [128, 512], BF16, 2)
        C["xwr"] = Rot(sb, "xw", [128, 24, 64], BF16, 1 if fwd else 2)
        if fwd:
            C["bTr"] = Rot(sb, "bTc", [128, 4, 128], BF16, 2); C["cTr"] = Rot(sb, "cTc", [128, 4, 128], BF16, 2)
            if isinstance(gc_n, int):
                C["gcr"] = Rot(sb, "gc", [128, 1536], BF16, gc_n)
            else:
                C["gcr"] = Rot.__new__(Rot); C["gcr"].b = list(gc_n); C["gcr"].i = 0
            C["cbm"] = [Rot(sb, "cbm", [128, 4, 128], F32, 2)]
            C["elr"] = Rot(sb, "el", [128, 3, 128], F32, el_n); C["mtr"] = Rot(sb, "mt", [128, 3, 128], BF16, 16)
            if pieces is not None:
                C["yT"], C["gy"], C["glbr"], C["ywr"] = pieces
            else:
                C["yT"] = scratch[:, 0:3072].bitcast(F32).rearrange("p (j t) -> p j t", t=128)
                C["gy"] = scratch[:, 3072:6144].bitcast(F32).rearrange("p (j t) -> p j t", t=128)
                C["glbr"] = [(scratch[:, 6144 + b_ * 1536:7680 + b_ * 1536].rearrange("p (j t) -> p j t", t=128), f"glb{b_}") for b_ in range(2)]
                C["ywr"] = []
            C["sqb"] = sb.alloc([128, 12, 128], BF16); C["r1"] = sb.alloc([128, 512], F32); C["rstd"] = sb.alloc([128, 512], F32)
            C["msb"] = Rot(sb, "msb", [128, 12, 128], BF16, 1)
        return C

    def load_chunk(ps, C, c, need_ct):
        nm = ps["name"]
        xs_, xr = C["xsr"].next(); bt_, br = C["btr"].next()
        P.dma("sp", xs_, ps["xs"][c * 128:(c + 1) * 128, :], [f"xs{nm}_{c}"], [xr])
        P.dma("sp", bt_, ps["btok"][c * 128:(c + 1) * 128, :], [f"btok{nm}_{c}"], [br])
        res = [xs_, xr, bt_, br]
        if need_ct:
            n = (c * 128) // ps["NT"]
            bT, bTr_ = C["bTr"].next(); cT, cTr_ = C["cTr"].next()
            P.dma("sp", bT, ps["bt"][:, :, c * 128:(c + 1) * 128].rearrange("g p t -> p g t"), [f"bt{nm}{g}_{n}" for g in range(4)], [bTr_])
            P.dma("sp", cT, ps["ct"][:, :, c * 128:(c + 1) * 128].rearrange("g p t -> p g t"), [f"ct{nm}{g}_{n}" for g in range(4)], [cTr_])
            res += [bT, bTr_, cT, cTr_]
        return res

    def make_xw(C, d, c, xs_, xr):
        xw, xwr_ = C["xwr"].next()
        P.tt("pool", xw, xs_.rearrange("p (h e) -> p h e", e=64), C["wtok"][:, c, d * 24:(d + 1) * 24].unsqueeze(2).to_broadcast([128, 24, 64]),
             ALU.mult, [xr, "wtok"], [xwr_])
        return xw, xwr_

    def state_combine(C, d, c, bt_, br, xw, xwr_):
        S = Sst[d]; SR = f"S{d}"
        for g in range(4):
            pb, pr = bank()
            P.mm(pb[:, :384], bt_[:, g * 128:(g + 1) * 128], xw[:, g * 6:(g + 1) * 6, :].rearrange("p h e -> p (h e)"), True, True, [br, xwr_], [pr])
            Sv = S[:, g * 384:(g + 1) * 384]
            P.tt("dve", Sv.rearrange("p (h e) -> p h e", e=64), Sv.rearrange("p (h e) -> p h e", e=64),
                 C["dtb"][:, c, d * 24 + g * 6:d * 24 + g * 6 + 6].unsqueeze(2).to_broadcast([128, 6, 64]), ALU.mult, [SR, "dtotbc"], [SR])
            P.tt("dve", Sv, Sv, pb[:, :384], ALU.add, [SR, pr], [SR])
        P.copy("act", Sbf[d], S, [SR], [f"Sb{d}"])

    def ssd_bwd_steps(ps, C):
        nch = ps["nch"]; nm = ps["name"]
        order = list(range(nch - 1, -1, -1))
        ld = {}; xws = {}

        def step(i):
            c = order[i]
            if i == 0:
                ld[c] = load_chunk(ps, C, c, False)
                xws[c] = make_xw(C, 1, c, ld[c][0], ld[c][1])
            ahead = len(C["xwr"].b) >= 2
            if i + 1 < nch and ahead:
                c1 = order[i + 1]
                ld[c1] = load_chunk(ps, C, c1, False)
                xws[c1] = make_xw(C, 1, c1, ld[c1][0], ld[c1][1])
            P.dma("pool", ps["gst"][c], Sbf[1], ["Sb1"], [f"gst{nm}{c}"])
            state_combine(C, 1, c, ld[c][2], ld[c][3], xws[c][0], xws[c][1])
            del ld[c], xws[c]
            if i + 1 < nch and not ahead:
                c1 = order[i + 1]
                ld[c1] = load_chunk(ps, C, c1, False)
                xws[c1] = make_xw(C, 1, c1, ld[c1][0], ld[c1][1])

        return [lambda i=i: step(i) for i in range(nch)]

    def ssd_fwd(l, ps, C, extra=None, tick=None):
        nch = ps["nch"]; nm = ps["name"]
        RL = C["RL"]; RS = C["RS"]; cbm = C["cbm"]; elr = C["elr"]; mtr = C["mtr"]; gcr = C["gcr"]
        yT = C["yT"]; gy = C["gy"]; glbr = C["glbr"]; sqb = C["sqb"]; r1 = C["r1"]; rstd = C["rstd"]; msb = C["msb"]
        o_snw, n_ = vo(l, "snw")
        extra = list(extra) if extra else []
        tk = tick if tick is not None else (lambda: None)
        ywr = C["ywr"]
        CH = {}

        def st_load(c):
            csl = slice(c * 128, (c + 1) * 128)
            xs_, xr, bt_, br, bT, bTr_, cT, cTr_ = load_chunk(ps, C, c, True)
            gc, gcr_ = gcr.next()
            P.dma("sp", gc, ps["gst"][c], [f"gst{nm}{c}"], [gcr_])
            gl_, glr_ = glbr[c % len(glbr)]
            P.dma("sp", gl_, ps["gates"][4:16, :, csl].rearrange("j p t -> p j t"),
                  [f"gate{nm}{j}_{(c * 128) // ps['NT']}" for j in range(4, 16)], [glr_] + ywr)
            CH[c] = dict(csl=csl, xs=xs_, xr=xr, bt=bt_, br=br, bT=bT, bTr=bTr_, cT=cT, cTr=cTr_, gc=gc, gcr=gcr_, gl=gl_, glr=glr_, tiles={}, yb=None)

        def st_A(c):
            k = CH[c]
            pbc, pcr = bank()
            for g in range(4):
                P.mm(pbc[:, g * 128:(g + 1) * 128], k["bT"][:, g, :], k["cT"][:, g, :], True, True, [k["bTr"], k["cTr"]], [pcr])
            t, tr_ = cbm[0].next()
            P.copy("act", t, pbc.rearrange("p (g t) -> p g t", g=4), [pcr], [tr_])
            k["cm"] = [(t, tr_), (t, tr_)]

        def st_T(c, g):
            k = CH[c]
            tiles = {}
            for d in range(2):
                for half in range(2):
                    for virt in range(2):
                        pb, pr = bank()
                        for i in range(3):
                            h = g * 6 + half * 3 + i
                            q = d * 24 + h
                            o_ = pb[:, i * 128:(i + 1) * 128]
                            P.mm(o_, esel[:, q, :], RL[:, k["csl"]], True, bool(virt), ["cbf", "RL"], [pr])
                            if not virt:
                                P.mm(o_, RS[:, k["csl"]], esel[:, q, :], False, False, ["cbf", "RS"], [pr])
                                P.mm(o_, ident, negm[d], False, True, ["cbf"], [pr])
                        el, elr_ = elr.next()
                        P.act(el, pb[:, :384].rearrange("p (i t) -> p i t", i=3), AF.Exp, [pr], [elr_])
                        mt, mtr_ = mtr.next()
                        if virt:
                            P.tt("pool", mt, el, k["cT"][:, g, :].unsqueeze(1).to_broadcast([128, 3, 128]), ALU.mult, [elr_, k["cTr"]], [mtr_])
                        else:
                            cm = k["cm"][d]
                            P.tt("dve", mt, el, cm[0][:, g, :].unsqueeze(1).to_broadcast([128, 3, 128]), ALU.mult, [elr_, cm[1]], [mtr_])
                        tiles[(d, half, virt)] = (mt, mtr_)
            k["tiles"][g] = tiles

        def st_Y(c, g):
            k = CH[c]
            tiles = k["tiles"][g]
            for jp in range(3):
                j = g * 3 + jp
                if j % 4 == 0:
                    k["yb"] = ybank_()
                pb, pr = k["yb"]
                for hh in range(2):
                    h = 2 * j + hh
                    hl = h - g * 6
                    half, i = hl // 3, hl % 3
                    co = hh * 64
                    o_ = pb[co:co + 64, (j % 4) * 128:(j % 4 + 1) * 128]
                    xh = k["xs"][:, h * 64:(h + 1) * 64]
                    seq = [(xh, tiles[(0, half, 0)], [k["xr"]]), (xh, tiles[(1, half, 0)], [k["xr"]]), (xh, (DI[:, h, :], "DI"), [k["xr"]]),
                           (Sbf[0][:, h * 64:(h + 1) * 64], tiles[(0, half, 1)], ["Sb0"]), (k["gc"][:, h * 64:(h + 1) * 64], tiles[(1, half, 1)], [k["gcr"]])]
                    for si, (lh, (rt, rtr), lr) in enumerate(seq):
                        rhs = rt if rtr == "DI" else rt[:, i, :]
                        P.mm(o_, lh, rhs, si == 0, si == len(seq) - 1, lr + [rtr], [pr], tile_position=(0, co))
                if j % 4 == 3:
                    P.copy("act", yT[:, j - 3:j + 1, :], pb.rearrange("p (k t) -> p k t", k=4), [pr], ["yT"] + ywr)

        def st_G1(c):
            k = CH[c]
            P.tt("pool", gy, yT, k["gl"], ALU.mult, ["yT", k["glr"]], ["gy"])
            P.act(sqb, gy, AF.Square, ["gy"], ["sqb"])

        def st_G2(c):
            pb, pr = bank()
            for g in range(4):
                for i in range(3):
                    P.mm(pb[:, g * 128:(g + 1) * 128], ones, sqb[:, g * 3 + i, :], i == 0, i == 2, ["cbf", "sqb"], [pr])
            P.act(r1, pb, AF.Ln, [pr], ["r1s"], bias=EPS, scale=1.0 / 384.0)
            P.act(rstd, r1, AF.Exp, ["r1s"], ["rstds"], scale=-0.5)

        def st_G3(c, csl):
            ms, msr = msb.next()
            for g in range(4):
                P.tt("dve", ms[:, g * 3:(g + 1) * 3, :], gy[:, g * 3:(g + 1) * 3, :], rstd[:, g * 128:(g + 1) * 128].unsqueeze(1).to_broadcast([128, 3, 128]),
                     ALU.mult, ["gy", "rstds"], [msr])
            P.dma("act", ps["mix"][4:16, :, csl].rearrange("j p t -> p j t"), ms, [msr], [f"mixs{nm}_{c}"])

        st_load(0)
        if nch > 1:
            st_load(1)
        st_A(0); st_T(0, 0); st_T(0, 1)
        pend = None
        for c in range(nch):
            k = CH[c]
            xw, xwr_ = make_xw(C, 0, c, k["xs"], k["xr"])
            st_Y(c, 0); tk(); st_T(c, 2); tk(); st_Y(c, 1); tk()
            if pend is not None:
                st_G3(*pend); tk()
            st_T(c, 3); tk(); st_Y(c, 2); tk()
            if c + 1 < nch:
                st_A(c + 1); tk(); st_T(c + 1, 0); tk()
            st_Y(c, 3); tk()
            st_G1(c); tk()
            state_combine(C, 0, c, k["bt"], k["br"], xw, xwr_); tk()
            if c + 1 < nch:
                st_T(c + 1, 1); tk()
            st_G2(c); tk()
            pend = (c, k["csl"])
            if c + 2 < nch:
                st_load(c + 2)
            del CH[c]
            if extra:
                extra.pop(0)()
            if len(extra) > nch - 1 - c and extra:
                extra.pop(0)()
        st_G3(*pend)
        while extra:
            extra.pop(0)()

    def wo_views():
        uflat_ = uT.rearrange("p c t -> p (c t)")
        zflat_ = zrow.bitcast(BF16)
        return [uflat_[:, k * 1024:(k + 1) * 1024] for k in range(9)] + [zflat_[:, k * 1024:(k + 1) * 1024] for k in range(4)]

    def wo_load_step(l, k, ap):
        wv = w_out[l].rearrange("(k p) j -> p k j", p=128)
        o_snw, n_ = vo(l, "snw")
        P.dma("pool", ap, wv[:, k, :], [], [f"wo{k}"])
        if k >= 4:
            P.ts("dve", ap, ap, vecs[:, o_snw + k - 4:o_snw + k - 3], None, ALU.mult, None, [f"wo{k}", "vecs"], [f"wo{k}"])

    def phase_outproj(l, last):
        sb.top = arena0
        fuse = not last
        ln = l + 1
        wo_k = wo_views() + [sb.alloc([128, D], BF16) for _ in range(3)]
        mixr = Rot(sb, "mixb", [128, 16, 512], BF16, 2)
        xor_ = Rot(sb, "xo", [128, 512], F32, 2); xnr = Rot(sb, "xn", [128, 512], F32, 10 if fuse else 2)
        if fuse:
            sqrot = Rot(sb, "sq", [128, 512], BF16, 2); r1 = sb.alloc([128, 512], F32); rstd = sb.alloc([128, 512], F32)
            tmpr = Rot(sb, "ntmp", [128, 512], F32, 2)
            for g in range(3):
                wb, wr = wrot.next()
                P.dma("pool", wb, w_in[ln].rearrange("(c p) j -> p c j", p=128)[:, :, g * 512:(g + 1) * 512], [], [wr])
                winpre[g] = (wb, wr)
        for k in range(13, 16):
            wo_load_step(l, k, wo_k[k])
        blocks = [(ps, n) for ps in ((px,) if last else (pc, px)) for n in range(ps["nblk"])]

        def emit_norm(xns, pn, pnr, NT, pi, hsl):
            P.act(r1[:, :NT], pn[:, :NT], AF.Ln, [pnr], ["r1"], bias=EPS, scale=1.0 / D)
            P.act(rstd[:, :NT], r1[:, :NT], AF.Exp, ["r1"], ["rstd"], scale=-0.5)
            for c in range(8):
                xn, xnr_ = xns[c]
                tm, tmr = tmpr.next()
                P.stt(tm[:, :NT], xn[:, :NT], Amod2[ln % 2][:, c, pi:pi + 1], rstd[:, :NT], ALU.mult, ALU.mult, [xnr_, f"Amod{ln % 2}", "rstd"], [tmr])
                P.act(hT[:, c, hsl], tm[:, :NT], AF.Identity, [tmr, f"modT{ln % 2}"], ["hT"], bias=modT2[ln % 2][:, c, pi:pi + 1], scale=1.0)

        pend_norm = None
        for ps, n in blocks:
            NT = ps["NT"]; nm = ps["name"]; pi = ps["pi"]
            sl = slice(n * NT, (n + 1) * NT)
            hsl = slice(ps["off"] + n * NT, ps["off"] + (n + 1) * NT)
            mb, mbr = mixr.next()
            rd = [f"mix{nm}{j}_{n}" for j in range(4)] + [f"mixs{nm}_{c}" for c in range(n * NT // 128, (n + 1) * NT // 128)]
            P.dma("sp", mb[:, :, :NT], ps["mix"][:, :, sl].rearrange("j p t -> p j t"), rd, [mbr])
            if fuse:
                pn, pnr = ybank_()
            xns = []
            pend_mm = None
            for dc in range(8):
                pb, pr = bank()
                for k in range(16):
                    P.mm(pb[:, :NT], wo_k[k][:, dc * 128:(dc + 1) * 128], mb[:, k, :NT], k == 0, k == 15, [f"wo{k}", mbr], [pr])
                if pend_mm is not None:
                    sq_, sr_, d_ = pend_mm
                    P.mm(pn[:, :NT], ones, sq_[:, :NT], d_ == 0, d_ == 7, [sr_, "cbf"], [pnr])
                    pend_mm = None
                xo, xor__ = xor_.next()
                P.dma("sp", xo[:, :NT], ps["res"][dc][:, sl], [f"res{nm}{dc}"], [xor__])
                xn, xnr_ = xnr.next()
                P.stt(xn[:, :NT], pb[:, :NT], modT2[l % 2][:, 16 + dc, pi:pi + 1], xo[:, :NT], ALU.mult, ALU.add, [pr, f"modT{l % 2}", xor__], [xnr_])
                P.dma("act", ps["res"][dc][:, sl], xn[:, :NT], [xnr_], [f"res{nm}{dc}"])
                if fuse:
                    sq, sr = sqrot.next()
                    P.act(sq[:, :NT], xn[:, :NT], AF.Square, [xnr_], [sr])
                    pend_mm = (sq, sr, dc)
                    xns.append((xn, xnr_))
                    if dc == 1 and pend_norm is not None:
                        emit_norm(*pend_norm)
                        pend_norm = None
            if fuse:
                sq_, sr_, d_ = pend_mm
                P.mm(pn[:, :NT], ones, sq_[:, :NT], d_ == 0, d_ == 7, [sr_, "cbf"], [pnr])
                pend_norm = (xns, pn, pnr, NT, pi, hsl)
        if pend_norm is not None:
            emit_norm(*pend_norm)
        P.barrier()

    def phase_final(ps):
        sb.top = arena0
        NT = ps["NT"]
        xrot = Rot(sb, "xb", [128, 8, NT], F32, 2); sqrot = Rot(sb, "sq", [128, NT], BF16, 2)
        r1 = sb.alloc([128, NT], F32); rstd = sb.alloc([128, NT], F32)
        yb = sb.alloc([128, 8, NT], F32); ost = Rot(sb, "ost", [128, D], F32, 2)
        o = DEPTH * PL
        for n in range(ps["nblk"]):
            sl = slice(n * NT, (n + 1) * NT)
            xb, xr = xrot.next()
            P.dma("sp", xb, ps["res"][:, :, sl].rearrange("c p t -> p c t"), [f"resx{c}" for c in range(8)], [xr])
            rms_block(ps, n, xb, xr, sqrot, r1, rstd, 8, float(D))
            for c in range(8):
                P.stt(yb[:, c, :], xb[:, c, :], vecs[:, o + c:o + c + 1], rstd, ALU.mult, ALU.mult, [xr, "vecs", "rstd"], ["yb"])
            for t in range(NT // 128):
                os_, osr = ost.next()
                for half in range(2):
                    pb, pr = bank()
                    for k in range(4):
                        c = half * 4 + k
                        P.tr(pb[:, k * 128:(k + 1) * 128], yb[:, c, t * 128:(t + 1) * 128], ident32, ["yb", "cf32"], [pr])
                    P.copy(alt(), os_[:, half * 512:(half + 1) * 512], pb, [pr], [osr])
                P.dma("pool", out[n * NT + t * 128:n * NT + (t + 1) * 128, :], os_, [osr], [])

    pc, px = passes["c"], passes["x"]
    load_resid(pc, ctx_in)
    load_resid(px, x_in)
    for st in adaln_steps(0):
        st()
    hflat = hT.rearrange("p c t -> p (c t)")
    uflat = uT.rearrange("p c t -> p (c t)")
    for l in range(depth):
        adaln_small(l)
        for d in range(2):
            P.add("pool", lambda e, d=d: e.memset(Sst[d], 0.0), [], [f"S{d}"])
            P.add("pool", lambda e, d=d: e.memset(Sbf[d], 0.0), [], [f"Sb{d}"])
        if l == 0:
            phase_norm(l)
        phase_inproj(l)
        Kx = ssd_common(px)
        Kc = ssd_common(pc, reset=False)
        ssd_prep(l, pc, Kc, roff=4 * SEQ)
        saved_ops = P.ops
        P.ops = []
        ssd_prep(l, px, Kx, roff=0)
        px_ops = P.ops
        P.ops = saved_ops

        def splice(k=2):
            n_ = 0
            while px_ops and (n_ < k or P.ops[-1][0] == "pe"):
                P.ops.append(px_ops.pop(0))
                n_ += 1

        splice(3)
        zf32 = zrow
        yT_c = zf32[:, 0:1536].rearrange("p (j t) -> p j t", t=128)
        glb_c = zf32[:, 1536:2304].bitcast(BF16).rearrange("p (j t) -> p j t", t=128)
        Cc_gy = None
        wflat = [wrot.b[i_][0].rearrange("p c j -> p (c j)") for i_ in range(2)]
        glb_c2 = wflat[0][:, 0:1536].rearrange("p (j t) -> p j t", t=128)
        gcs = [(wflat[0][:, 1536:3072], "gcc0"), (wflat[1][:, 0:1536], "gcc1")]
        Cc = ssd_ctx(pc, Kc, True, gc_n=gcs, el_n=2, pieces=(yT_c, None, [(glb_c, "glbc0"), (glb_c2, "glbc1")], ["zrow"]))
        Cc["gy"] = sb.alloc([128, 12, 128], F32)
        for st in ssd_bwd_steps(pc, Cc):
            st()
            splice(2)
        ssd_fwd(l, pc, Cc, tick=splice)
        while px_ops:
            splice(4)
        P.barrier()
        Kx = ssd_common(px)
        Cb = ssd_ctx(px, Kx, False)
        F = fft_bufs(l)
        bankpool[0] = [0, 1, 2, 3]
        bsteps = ssd_bwd_steps(px, Cb)

        def tick():
            if bsteps:
                bsteps.pop(0)()

        phase_fft(l, pc, F, tick, fixed=True)
        phase_fft(l, px, F, tick, fixed=True)
        while bsteps:
            bsteps.pop(0)()
        bankpool[0] = list(range(6))
        P.barrier()
        Kx = ssd_common(px)
        Cf = ssd_ctx(px, Kx, True, scratch=hflat[:, 0:9216])
        wov = wo_views()
        ex = (adaln_steps(l + 1) if l + 1 < depth else []) + [(lambda k=k: wo_load_step(l, k, wov[k])) for k in range(13)]
        ssd_fwd(l, px, Cf, extra=ex)
        P.barrier()
        phase_outproj(l, l == depth - 1)
    phase_final(px)
    P.emit()
    nc._sb_peak = sb.peak
    nc._nops = len(P.ops)
    return nc


_CACHE = {}


def _consts():
    if "c" in _CACHE:
        return _CACHE["c"]
    bf = ml_dtypes.bfloat16
    cbf = np.zeros((128, 512 + 48 * 128 + 256), np.float32)
    cbf[:, 0:128] = np.eye(128)
    cbf[:, 128:256] = 1.0
    k = np.arange(64)
    ang = 2 * np.pi * np.outer(k, k) / 64.0
    C64, S64 = np.cos(ang), np.sin(ang)
    z = np.zeros((64, 64))
    cbf[:, 256:384] = np.block([[C64, z], [z, C64]])
    cbf[:, 384:512] = np.block([[S64, z], [z, S64]])
    es = np.zeros((128, 48, 128), np.float32)
    for q in range(48):
        es[q, q, :] = 1.0
        es[64 + q, q, :] = 1.0
    cbf[:, 512:512 + 48 * 128] = es.reshape(128, -1)
    cf = np.zeros((128, 516), np.float32)
    cf[:, 0:128] = np.eye(128)
    s = np.arange(128)[:, None]; l_ = np.arange(128)[None, :]
    cbf[:, 512 + 48 * 128:512 + 48 * 128 + 128] = np.where(l_ >= s, 0.0, -30000.0)
    cbf[:, 512 + 48 * 128 + 128:] = np.where(l_ <= s, 0.0, -30000.0)
    cf[:, 128:256] = (l_ >= s)
    cf[:, 256:384] = (l_ <= s)
    cf[:, 384:512] = 1.0
    rows = np.arange(128) % 64
    valid = rows < 48
    cf[:, 512] = (valid & (rows < 24))
    cf[:, 513] = (valid & (rows >= 24))
    cf[:, 514] = cf[:, 512] - cf[:, 513]
    dft = {}
    for L in (2048, 256):
        t = np.arange(L, dtype=np.float64)
        a = 2 * np.pi * ((np.outer(t, t)) % L) / L
        sc = 1.0 / np.sqrt(64.0 * L)
        dft[L] = ((np.cos(a) * sc).astype(np.float32).astype(bf), (-np.sin(a) * sc).astype(np.float32).astype(bf))
    _CACHE["c"] = (cbf.astype(bf), cf, dft)
    return _CACHE["c"]


def _pp(v):
    return np.ascontiguousarray(v.reshape(-1, 128).T)


def _vecs(b, c, c_ctx, norm_w, b_ada, conv_w, conv_b, dt_bias, a_log, d_skip, ssd_norm_w, b_fourier, final_norm_w):
    v = np.zeros((128, NV), np.float32)
    for l in range(DEPTH):
        o = l * PL
        v[:, o:o + 8] = _pp(norm_w[l]); v[:, o + 8:o + 32] = _pp(b_ada[l]); v[:, o + 32:o + 52] = _pp(conv_b[l])
        cw = conv_w[l].reshape(9, 2560)
        for t in range(9):
            v[:, o + 52 + t * 20:o + 52 + (t + 1) * 20] = _pp(cw[t])
        for half in (0, 64):
            v[half:half + 48, o + 232] = dt_bias[l].reshape(48)
            v[half:half + 48, o + 233] = a_log[l].reshape(48)
        v[:, o + 234:o + 258] = d_skip[l][None, :]
        v[:, o + 258:o + 270] = _pp(ssd_norm_w[l]); v[:, o + 270:o + 274] = _pp(b_fourier[l])
    o = DEPTH * PL
    v[:, o:o + 8] = _pp(final_norm_w); v[:, o + 8:o + 16] = _pp(c[b]); v[:, o + 16:o + 24] = _pp(c_ctx)
    return v


def make_in_maps(inputs, ncores=8):
    f = lambda a: np.ascontiguousarray(np.asarray(a, dtype=np.float32))
    I = {k: f(v) for k, v in inputs.items()}
    cbf, cf, dft = _consts()
    maps = []
    for b in range(ncores):
        maps.append({
            "x": I["x"][b], "ctx": I["ctx"][b],
            "vecs": _vecs(b, I["c"], I["c_ctx"], I["norm_w"], I["b_ada"], I["conv_w"], I["conv_b"], I["dt_bias"], I["a_log"],
                          I["d_skip"], I["ssd_norm_w"], I["b_fourier"], I["final_norm_w"]),
            "w_ada": I["w_ada"], "w_in": I["w_in"], "w_fourier": I["w_fourier"], "w_out": I["w_out"], "conv_b": I["conv_b"],
            "cbf": cbf, "cf32": cf, "cl2048": dft[2048][0], "sl2048": dft[2048][1], "cl256": dft[256][0], "sl256": dft[256][1],
        })
    return maps


def kernel(**inputs):
    if "nc" not in _CACHE:
        _CACHE["nc"] = build()
    nc = _CACHE["nc"]
    maps = make_in_maps(inputs)
    res = run_bass_kernel_spmd(nc, maps, core_ids=list(range(8)))
    return np.stack([np.asarray(r["out"], dtype=np.float32) for r in res.results], axis=0)
```

```python
import numpy as np
import concourse.bass as bass
import concourse.mybir as mybir
from concourse.bass_utils import run_bass_kernel_spmd

F32 = mybir.dt.float32
BF16 = mybir.dt.bfloat16
AF = mybir.ActivationFunctionType
ALU = mybir.AluOpType

ENGS = ["pe", "act", "dve", "pool", "sp"]
NDS = 8
SAME_ENG_WAIT = False


class Prog:
    def __init__(self, nc):
        self.nc = nc
        self.ops = []

    def add(self, eng, fn, reads=(), writes=(), dma=False):
        self.ops.append((eng, fn, tuple(reads), tuple(writes), dma))

    def barrier(self):
        self.ops.append(("bar", None, (), (), False))

    def mm(self, out, lhsT, rhs, start, stop, reads, writes, **kw):
        self.add("pe", lambda e: e.matmul(out, lhsT, rhs, start=start, stop=stop, **kw), reads, writes)

    def tr(self, out, in_, ident, reads, writes):
        self.add("pe", lambda e: e.transpose(out, in_, ident), reads, writes)

    def act(self, out, in_, func, reads, writes, **kw):
        self.add("act", lambda e: e.activation(out, in_, func, **kw), reads, writes)

    def tt(self, eng, out, in0, in1, op, reads, writes):
        self.add(eng, lambda e: e.tensor_tensor(out, in0, in1, op), reads, writes)

    def ts(self, eng, out, in0, s1, s2, op0, op1, reads, writes):
        if op1 is None:
            self.add(eng, lambda e: e.tensor_scalar(out, in0, s1, None, op0), reads, writes)
        else:
            self.add(eng, lambda e: e.tensor_scalar(out, in0, s1, s2, op0, op1), reads, writes)

    def stt(self, out, in0, scalar, in1, op0, op1, reads, writes):
        self.add("dve", lambda e: e.scalar_tensor_tensor(out, in0, scalar, in1, op0, op1), reads, writes)

    def copy(self, eng, out, in_, reads, writes):
        if eng == "act":
            self.add("act", lambda e: e.copy(out, in_), reads, writes)
        else:
            self.add(eng, lambda e: e.tensor_copy(out, in_), reads, writes)

    def dma(self, eng, out, in_, reads, writes, **kw):
        self.add(eng, lambda e: e.dma_start(out, in_, **kw), reads, writes, dma=True)

    def emit(self):
        nc = self.nc
        ops = self.ops
        n = len(ops)
        cnt = {e: 0 for e in ENGS}
        sig = [None] * n
        slot = {e: 0 for e in ENGS}
        dcount = {}
        prevuse = [None] * n
        last_w = {}
        readers = {}
        deps = [None] * n
        for i, (eng, fn, r, w, dma) in enumerate(ops):
            if eng == "bar":
                deps[i] = set()
                continue
            d = set()
            for x in r:
                if x in last_w:
                    d.add(last_w[x])
            for x in w:
                if x in last_w:
                    d.add(last_w[x])
                rd = readers.get(x)
                if rd:
                    d.update(rd.values())
            d.discard(i)
            deps[i] = d
            for x in r:
                readers.setdefault(x, {})[("d", i) if dma else eng] = i
            for x in w:
                last_w[x] = i
                readers[x] = {}
        needed = set()
        for i in range(n):
            needed |= deps[i]
        lastc = {}
        for i, (eng, fn, r, w, dma) in enumerate(ops):
            if eng == "bar":
                needed.update(lastc.values())
            elif not dma:
                lastc[eng] = i
        needed.update(lastc.values())
        snaps = {}
        for i, (eng, fn, r, w, dma) in enumerate(ops):
            if eng == "bar":
                snaps[i] = (dict(cnt), dict(dcount))
                continue
            if dma:
                k = slot[eng] % NDS
                slot[eng] += 1
                key = (eng, k)
                prev = dcount.get(key, 0)
                if prev > 0:
                    prevuse[i] = (key, prev)
                dcount[key] = prev + 16
                sig[i] = ("d", key, prev + 16)
            elif i in needed:
                cnt[eng] += 1
                sig[i] = ("c", eng, cnt[eng])
        sems = {}
        for e in ENGS:
            if cnt[e] > 0:
                sems[("c", e)] = nc.alloc_semaphore(f"c_{e}")
        for key in dcount:
            sems[("d", key)] = nc.alloc_semaphore(f"d_{key[0]}{key[1]}")
        self.n_ops = n
        by_eng = {e: [i for i in range(n) if ops[i][0] == e or ops[i][0] == "bar"] for e in ENGS}

        def run(eng, eobj):
            waited = {}
            for i in by_eng[eng]:
                _, fn, r, w, dma = ops[i]
                if ops[i][0] == "bar":
                    c_snap, d_snap = snaps[i]
                    for f_, v_ in c_snap.items():
                        if v_ > 0 and f_ != eng and waited.get(("c", f_), 0) < v_:
                            eobj.wait_ge(sems[("c", f_)], v_)
                            waited[("c", f_)] = v_
                    for k_, v_ in d_snap.items():
                        if v_ > 0 and waited.get(("d", k_), 0) < v_:
                            eobj.wait_ge(sems[("d", k_)], v_)
                            waited[("d", k_)] = v_
                    continue
                need = {}
                for d in deps[i]:
                    s = sig[d]
                    if s[0] == "c":
                        if s[1] == eng and not dma and (eng == "pe" or not SAME_ENG_WAIT):
                            continue
                        key = ("c", s[1])
                    else:
                        key = ("d", s[1])
                    if need.get(key, 0) < s[2]:
                        need[key] = s[2]
                if prevuse[i] is not None:
                    key = ("d", prevuse[i][0])
                    if need.get(key, 0) < prevuse[i][1]:
                        need[key] = prevuse[i][1]
                for key, val in need.items():
                    if waited.get(key, 0) < val:
                        eobj.wait_ge(sems[key], val)
                        waited[key] = val
                ins = fn(eobj)
                s = sig[i]
                if s is None:
                    pass
                elif s[0] == "c":
                    ins.then_inc(sems[("c", eng)], 1)
                else:
                    ins.then_inc(sems[("d", s[1])], 16)
            for key, val in dcount.items():
                if key[0] == eng and waited.get(("d", key), 0) < val:
                    eobj.wait_ge(sems[("d", key)], val)

        with nc.Block() as block:

            @block.tensor
            def _(e):
                run("pe", e)

            @block.scalar
            def _(e):
                run("act", e)

            @block.vector
            def _(e):
                run("dve", e)

            @block.gpsimd
            def _(e):
                run("pool", e)

            @block.sync
            def _(e):
                run("sp", e)


import ml_dtypes

D = 1024; DEPTH = 4; SEQ = 2048; CTXL = 256
DPROJ = 5168; NH = 24; EPS = 1e-6
PL = 274
NV = DEPTH * PL + 24
BIG = 3.0e38


class SB:
    def __init__(self, nc):
        self.nc = nc
        self.top = ((nc.sbuf_base + 63) // 64) * 64
        self.lim = nc.sbuf_top
        self.k = 0
        self.peak = 0

    def alloc(self, shape, dt):
        nbytes = int(np.prod(shape[1:])) * (4 if dt == F32 else 2)
        off = self.top
        self.top = ((off + nbytes + 63) // 64) * 64
        self.peak = max(self.peak, self.top)
        assert self.top <= self.lim, f"SBUF overflow {self.top} > {self.lim}"
        self.k += 1
        return self.nc.alloc_sbuf_tensor_at(f"t{self.k}", list(shape), dt, offset=off).ap()


class Rot:
    def __init__(self, sb, name, shape, dt, n):
        self.b = [(sb.alloc(shape, dt), f"{name}{i}") for i in range(n)]
        self.i = 0

    def next(self):
        r = self.b[self.i % len(self.b)]
        self.i += 1
        return r


def vo(l, what):
    base = l * PL
    return {"nw": (base, 8), "bada": (base + 8, 24), "convb": (base + 32, 20), "cw": (base + 52, 180),
            "dtb": (base + 232, 1), "alog": (base + 233, 1), "dskip": (base + 234, 24),
            "snw": (base + 258, 12), "bf": (base + 270, 4)}[what]


def build(depth=DEPTH, dbg=False):
    nc = bass.Bass("TRN2", target_bir_lowering=False)
    din = lambda name, shape, dt=F32: nc.dram_tensor(name, list(shape), dt, kind="ExternalInput").ap()
    x_in = din("x", [SEQ, D]); ctx_in = din("ctx", [CTXL, D])
    vecs_in = din("vecs", [128, NV])
    w_ada = din("w_ada", [DEPTH, D, 3 * D]); w_in = din("w_in", [DEPTH, D, DPROJ])
    w_f = din("w_fourier", [DEPTH, 512, 512]); w_out = din("w_out", [DEPTH, 2048, D])
    conv_b = din("conv_b", [DEPTH, 2560])
    NCB = 512 + 48 * 128
    cbf_in = din("cbf", [128, NCB], BF16); cf32_in = din("cf32", [128, 4 * 128 + 4])
    dftin = {2048: (din("cl2048", [2048, 2048], BF16), din("sl2048", [2048, 2048], BF16)),
             256: (din("cl256", [256, 256], BF16), din("sl256", [256, 256], BF16))}
    out = nc.dram_tensor("out", [SEQ, D], F32, kind="ExternalOutput").ap()
    dscr = lambda name, shape, dt: (nc.dram_tensor(name, list(shape), dt, kind="ExternalOutput").ap() if dbg
                                    else nc.dram_tensor(name, list(shape), dt).ap())
    dbgout = {}

    P = Prog(nc)
    sb = SB(nc)
    PS = [nc.alloc_psum_tensor(f"ps{k}", [128, 512], F32).ap() for k in range(8)]
    bk = [0]

    bankpool = [list(range(6))]

    def bank():
        k = bankpool[0][bk[0] % len(bankpool[0])]
        bk[0] += 1
        return PS[k], f"ps{k}"

    yk = [0]

    def ybank_():
        k = 6 + yk[0] % 2
        yk[0] += 1
        return PS[k], f"ps{k}"

    cbf = sb.alloc([128, NCB], BF16)
    cf32 = sb.alloc([128, 516], F32)
    vecs = sb.alloc([128, NV], F32)
    ident = cbf[:, 0:128]; ones = cbf[:, 128:256]; bdcs = cbf[:, 256:512]
    esel = cbf[:, 512:NCB].rearrange("p (q s) -> p q s", q=48)
    ident32 = cf32[:, 0:128]; maskd = [cf32[:, 128:256], cf32[:, 256:384]]; ones32 = cf32[:, 384:512]
    isf = cf32[:, 512:513]; isb = cf32[:, 513:514]; sgn = cf32[:, 514:515]
    resetm = sb.alloc([128, SEQ], BF16)
    sc2 = sb.alloc([128, 8, 2], BF16)
    modT2 = [sb.alloc([128, 24, 2], F32) for _ in range(2)]; Amod2 = [sb.alloc([128, 8, 2], F32) for _ in range(2)]
    Acol = sb.alloc([128, 1], F32)
    wrot = Rot(sb, "wb", [128, 8, 512], BF16, 3)
    TOT = SEQ + CTXL
    hT = sb.alloc([128, 8, TOT], BF16)
    uT = sb.alloc([128, 4, TOT], BF16)
    zrow = sb.alloc([128, TOT], F32)
    Sst = [sb.alloc([128, 1536], F32) for _ in range(2)]
    Sbf = [sb.alloc([128, 1536], BF16) for _ in range(2)]
    DI = sb.alloc([128, 24, 128], BF16)
    arena0 = sb.top

    P.dma("sp", cbf, cbf_in, [], ["cbf"])
    P.dma("sp", cf32, cf32_in, [], ["cf32"])
    P.dma("sp", vecs, vecs_in, [], ["vecs"])
    P.add("pool", lambda e: e.memset(resetm, 1.0), [], ["resetm"])
    P.add("pool", lambda e: e.memset(resetm.rearrange("p (c t) -> p c t", t=128)[:, :, 0:1], 0.0), ["resetm"], ["resetm"])
    P.add("pool", lambda e: e.memset(zrow, 1.0), [], ["zrow"])
    P.act(sc2[:, :, 0], vecs[:, NV - 16:NV - 8], AF.Silu, ["vecs"], ["sc2"])
    P.act(sc2[:, :, 1], vecs[:, NV - 8:NV], AF.Silu, ["vecs"], ["sc2"])

    passes = {}
    for name, L in (("c", CTXL), ("x", SEQ)):
        NT = min(512, L)
        passes[name] = dict(
            name=name, L=L, NT=NT, nblk=L // NT, nch=L // 128, pi=(1 if name == "c" else 0),
            res=dscr(f"res_{name}", [8, 128, L], F32), gates=dscr(f"gates_{name}", [16, 128, L], BF16),
            xs=dscr(f"xs_{name}", [L, 1536], BF16), btok=dscr(f"btok_{name}", [L, 512], BF16),
            bt=dscr(f"bt_{name}", [4, 128, L], BF16), ct=dscr(f"ct_{name}", [4, 128, L], BF16),
            mix=dscr(f"mix_{name}", [16, 128, L], BF16), gst=dscr(f"gst_{name}", [L // 128, 128, 1536], BF16),
            off=(SEQ if name == "c" else 0), R=(1 if name == "c" else 32), W=(256 if name == "c" else 64), dft=dftin[L])
    flip = [0]

    def alt():
        flip[0] ^= 1
        return "act" if flip[0] else "dve"

    def load_resid(ps, src):
        sb.top = arena0
        xl = Rot(sb, "xl", [128, D], F32, 2); xs_ = Rot(sb, "xst", [128, 4, 128], F32, 2)
        for tt in range(ps["nch"]):
            t, tr = xl.next()
            P.dma("sp", t, src[tt * 128:(tt + 1) * 128, :], [], [tr])
            for half in range(2):
                pb, pr = bank()
                for k in range(4):
                    c = half * 4 + k
                    P.tr(pb[:, k * 128:(k + 1) * 128], t[:, c * 128:(c + 1) * 128], ident32, [tr, "cf32"], [pr])
                st, sr = xs_.next()
                P.copy(alt(), st, pb.rearrange("p (k t) -> p k t", k=4), [pr], [sr])
                dst = ps["res"][half * 4:half * 4 + 4, :, tt * 128:(tt + 1) * 128].rearrange("c p t -> p c t")
                P.dma("pool", dst, st, [sr], [f"res{ps['name']}{half * 4 + k}" for k in range(4)])
        P.barrier()

    def adaln_steps(l):
        modT = modT2[l % 2]; Amod = Amod2[l % 2]; mr = f"modT{l % 2}"; ar = f"Amod{l % 2}"
        steps = []

        def grp(g):
            wb, wr = wrot.next()
            P.dma("pool", wb, w_ada[l].rearrange("(c p) j -> p c j", p=128)[:, :, g * 512:(g + 1) * 512], [], [wr])
            pb, pr = bank()
            for jj in range(4):
                for c in range(8):
                    P.mm(pb[:, jj * 2:jj * 2 + 2], wb[:, c, jj * 128:(jj + 1) * 128], sc2[:, c, :], c == 0, c == 7, [wr, "sc2"], [pr])
            o, n_ = vo(l, "bada")
            P.tt("dve", modT[:, g * 4:(g + 1) * 4, :], pb[:, 0:8].rearrange("p (j t) -> p j t", t=2),
                 vecs[:, o + g * 4:o + g * 4 + 4].unsqueeze(2).to_broadcast([128, 4, 2]), ALU.add, [pr, "vecs"], [mr])

        def fin():
            o, n_ = vo(l, "nw")
            P.stt(Amod, modT[:, 8:16, :], 1.0, vecs[:, o:o + 8].unsqueeze(2).to_broadcast([128, 8, 2]), ALU.add, ALU.mult, [mr, "vecs"], [ar])

        for g in range(6):
            steps.append(lambda g=g: grp(g))
        steps.append(fin)
        return steps

    def adaln_small(l):
        o, n_ = vo(l, "alog")
        P.act(Acol, vecs[:, o:o + 1], AF.Exp, ["vecs"], ["Acol"])
        P.ts("dve", Acol, Acol, -1.0, None, ALU.mult, None, ["Acol"], ["Acol"])
        o, n_ = vo(l, "dskip")
        P.tt("pool", DI, ident.unsqueeze(1).to_broadcast([128, 24, 128]), vecs[:, o:o + 24].unsqueeze(2).to_broadcast([128, 24, 128]),
             ALU.mult, ["cbf", "vecs"], ["DI"])

    def rms_block(ps, n, xb, xr, sqrot, r1, rstd, nfeat_chunks, denom):
        NT = ps["NT"]
        pb, pr = bank()
        for c in range(nfeat_chunks):
            sq, sr = sqrot.next()
            P.act(sq[:, :NT], xb[:, c, :NT], AF.Square, [xr], [sr])
            P.mm(pb[:, :NT], ones, sq[:, :NT], c == 0, c == nfeat_chunks - 1, [sr, "cbf"], [pr])
        P.act(r1[:, :NT], pb[:, :NT], AF.Sqrt, [pr], ["r1"], bias=EPS, scale=1.0 / denom)
        P.add("dve", lambda e: e.reciprocal(rstd[:, :NT], r1[:, :NT]), ["r1"], ["rstd"])

    winpre = {}

    def phase_norm(l):
        sb.top = arena0
        for g in range(3):
            wb, wr = wrot.next()
            P.dma("pool", wb, w_in[l].rearrange("(c p) j -> p c j", p=128)[:, :, g * 512:(g + 1) * 512], [], [wr])
            winpre[g] = (wb, wr)
        xrot = Rot(sb, "xb", [128, 8, 512], F32, 2); sqrot = Rot(sb, "sq", [128, 512], BF16, 2)
        r1 = sb.alloc([128, 512], F32); rstd = sb.alloc([128, 512], F32); tmpr = Rot(sb, "ntmp", [128, 512], F32, 2)
        for ps in (pc, px):
            NT = ps["NT"]; pi = ps["pi"]
            for n in range(ps["nblk"]):
                sl = slice(n * NT, (n + 1) * NT)
                hsl = slice(ps["off"] + n * NT, ps["off"] + (n + 1) * NT)
                xb, xr = xrot.next()
                P.dma("sp", xb[:, :, :NT], ps["res"][:, :, sl].rearrange("c p t -> p c t"), [f"res{ps['name']}{c}" for c in range(8)], [xr])
                rms_block(ps, n, xb, xr, sqrot, r1, rstd, 8, float(D))
                for c in range(8):
                    tm, tmr = tmpr.next()
                    P.stt(tm[:, :NT], xb[:, c, :NT], Amod2[l % 2][:, c, pi:pi + 1], rstd[:, :NT], ALU.mult, ALU.mult, [xr, f"Amod{l % 2}", "rstd"], [tmr])
                    P.act(hT[:, c, hsl], tm[:, :NT], AF.Identity, [tmr, f"modT{l % 2}"], ["hT"], bias=modT2[l % 2][:, c, pi:pi + 1], scale=1.0)
        P.barrier()

    def conv_block(l, ps, cb, xsh, xpr, dg, dgr, cstage, tstage):
        L = ps["L"]; NT = ps["NT"]; W = ps["W"]; nm = ps["name"]
        taps = [(dr, dc) for dr in ((-1, 0, 1) if ps["R"] > 1 else (0,)) for dc in (-1, 0, 1)]
        win = lambda dr, dc, t0, nt: xsh[dc + 1][:, (1 + dr) * W + t0:(1 + dr) * W + t0 + nt]
        tapi = lambda dr, dc: (dr + 1) * 3 + (dc + 1)
        o2, n_ = vo(l, "convb")

        def tstage_(n, cs, csr):
            dstt = ps["xs"] if cb < 12 else ps["btok"]
            ch0 = cb * 128 if cb < 12 else (cb - 12) * 128
            k4 = NT // 128
            pt, ptr = bank()
            ptb = pt.bitcast(BF16)
            for k in range(k4):
                P.tr(ptb[:, k * 128:(k + 1) * 128], cs[:, k * 128:(k + 1) * 128], ident, [csr, "cbf"], [ptr])
            ts_, tsr = tstage.next()
            P.copy("dve", ts_[:, :k4, :], ptb[:, :k4 * 128].rearrange("p (k c) -> p k c", c=128), [ptr], [tsr])
            d_ = dstt[n * NT:(n + 1) * NT, ch0:ch0 + 128].rearrange("(k p) c -> p k c", p=128)
            P.dma("sp", d_, ts_[:, :k4, :], [tsr], [f"{'xs' if cb < 12 else 'btok'}{nm}_{tt_}" for tt_ in range(n * k4, (n + 1) * k4)])

        prev = None
        for n in range(ps["nblk"]):
            pb, pr = bank()
            for ti, (dr, dc) in enumerate(taps):
                P.mm(pb[:, :NT], dg[:, tapi(dr, dc), :], win(dr, dc, n * NT, NT), ti == 0, ti == len(taps) - 1, [dgr, xpr], [pr])
            cs, csr = cstage.next()
            P.act(cs[:, :NT], pb[:, :NT], AF.Silu, [pr, "vecs"], [csr], bias=vecs[:, o2 + cb:o2 + cb + 1], scale=1.0)
            if cb >= 12:
                dst = ps["bt"] if cb < 16 else ps["ct"]
                P.dma("sp", dst[cb % 4][:, n * NT:(n + 1) * NT], cs[:, :NT], [csr], [f"{'bt' if cb < 16 else 'ct'}{nm}{cb % 4}_{n}"])
            if cb < 16:
                if prev is not None:
                    tstage_(*prev)
                prev = (n, cs, csr)
        if prev is not None:
            tstage_(*prev)

    def phase_inproj(l):
        sb.top = arena0
        gstage = Rot(sb, "gs", [128, 512], BF16, 3); dgrot = Rot(sb, "dg", [128, 9, 128], BF16, 2)
        cstage = Rot(sb, "cs", [128, 512], BF16, 2); tstage = Rot(sb, "tst", [128, 4, 128], BF16, 3)
        xsh = {}
        for ps in (pc, px):
            xsh[ps["name"]] = [[(sb.alloc([128, (ps["R"] + 2) * ps["W"]], BF16), f"xsh{ps['name']}{b_}") for _ in range(3)] for b_ in range(2)]
            for b_ in range(2):
                for (t, r) in xsh[ps["name"]][b_]:
                    P.add("pool", lambda e, t=t: e.memset(t, 0.0), [], [r])
        wv = w_in[l].rearrange("(c p) j -> p c j", p=128)
        P.add("pool", lambda e: e.memset(zrow, 1.0), [], ["zrow"])
        blocks = [(ps, n) for ps in (pc, px) for n in range(ps["nblk"])]
        xi = 0
        pendc = []
        for g in range(11):
            width = 512 if g < 10 else 48
            if g in winpre:
                wb, wr = winpre.pop(g)
            else:
                wb, wr = wrot.next()
                P.dma("pool", wb[:, :, :width], wv[:, :, g * 512:g * 512 + width], [], [wr])
            if g == 10:
                o, n_ = vo(l, "dtb")
                for ps, n in blocks:
                    NT = ps["NT"]
                    hsl = slice(ps["off"] + n * NT, ps["off"] + (n + 1) * NT)
                    pb, pr = bank()
                    for c in range(8):
                        P.mm(pb[0:48, :NT], wb[:, c, 0:48], hT[:, c, hsl], c == 0, c == 7, [wr, "hT"], [pr])
                    for c in range(8):
                        P.mm(pb[64:112, :NT], wb[:, c, 0:48], hT[:, c, hsl], c == 0, c == 7, [wr, "hT"], [pr], tile_position=(0, 64))
                    P.act(zrow[0:48, hsl], pb[0:48, :NT], AF.Exp, [pr, "vecs"], ["zrow"], bias=vecs[0:48, o:o + 1], scale=1.0)
                    P.act(zrow[64:112, hsl], pb[64:112, :NT], AF.Exp, [pr, "vecs"], ["zrow"], bias=vecs[64:112, o:o + 1], scale=1.0)
                    if pendc:
                        pendc.pop(0)()
                continue
            for jj in range(4):
                jb = g * 4 + jj
                if jb >= 20:
                    xi += 1
                for ps, n in blocks:
                    NT = ps["NT"]; nm = ps["name"]; W = ps["W"]
                    sl = slice(n * NT, (n + 1) * NT)
                    hsl = slice(ps["off"] + n * NT, ps["off"] + (n + 1) * NT)
                    pb, pr = bank()
                    for c in range(8):
                        P.mm(pb[:, :NT], wb[:, c, jj * 128:(jj + 1) * 128], hT[:, c, hsl], c == 0, c == 7, [wr, "hT"], [pr])
                    if jb < 4:
                        P.copy(alt(), uT[:, jb, hsl], pb[:, :NT], [pr], [f"uT{jb}"])
                    elif jb < 20:
                        gs, gr = gstage.next()
                        P.act(gs[:, :NT], pb[:, :NT], AF.Silu, [pr], [gr])
                        P.dma("sp", ps["gates"][jb - 4][:, sl], gs[:, :NT], [gr], [f"gate{nm}{jb - 4}_{n}"])
                    else:
                        xs3 = xsh[nm][xi % 2]
                        nr = NT // W
                        r0 = 1 + n * nr
                        src = pb[:, :NT].rearrange("p (r w) -> p r w", w=W)
                        v3 = lambda t: t.rearrange("p (r w) -> p r w", w=W)[:, r0:r0 + nr, :]
                        P.copy("dve", v3(xs3[1][0]), src, [pr], [xs3[1][1]])
                        P.copy("act", v3(xs3[0][0])[:, :, 1:W], src[:, :, 0:W - 1], [pr], [xs3[0][1]])
                        P.copy("dve", v3(xs3[2][0])[:, :, 0:W - 1], src[:, :, 1:W], [pr], [xs3[2][1]])
                if pendc:
                    pendc.pop(0)()
                if jb >= 20:
                    def do_conv(jb=jb, xi_=xi):
                        dg, dgr = dgrot.next()
                        o, n_ = vo(l, "cw")
                        cwv = vecs[:, o + (jb - 20):o + 180:20]
                        P.tt("pool", dg, ident.unsqueeze(1).to_broadcast([128, 9, 128]), cwv.unsqueeze(2).to_broadcast([128, 9, 128]),
                             ALU.mult, ["cbf", "vecs"], [dgr])
                        for ps in (pc, px):
                            xs3 = xsh[ps["name"]][xi_ % 2]
                            conv_block(l, ps, jb - 20, [t for (t, r) in xs3], xs3[0][1], dg, dgr, cstage, tstage)
                    pendc.append(do_conv)
        while pendc:
            pendc.pop(0)()
        P.barrier()

    def fft_bufs(l):
        F = dict(dfr=Rot(sb, "dft", [128, 2, 4, 512], BF16, 2), mixedT=sb.alloc([128, 4, 512], BF16), wf=sb.alloc([128, 4, 512], BF16),
                 glr=Rot(sb, "gl", [128, 512], BF16, 2), ftmp=Rot(sb, "ftmp", [128, 512], F32, 2), mst=Rot(sb, "mst", [128, 512], BF16, 2))
        P.dma("pool", F["wf"], w_f[l].rearrange("(c p) j -> p c j", p=128), [], ["wf"])
        return F

    def phase_fft(l, ps, F, tick=None, fixed=False):
        L = ps["L"]; NT = ps["NT"]; nch = ps["nch"]
        V = hT.rearrange("p c t -> p (c t)")[:, :nch * 1024].rearrange("p (t j s c) -> p t j s c", j=4, s=2, c=128)
        TG = min(4, nch)
        dfr = F["dfr"]; mixedT = F["mixedT"]; wf = F["wf"]; glr = F["glr"]; ftmp = F["ftmp"]; mst = F["mst"]
        tk = tick if tick is not None else (lambda: None)
        for tt in range(nch):
            for hf in range(2):
                pb, pr = bank()
                for jj in range(2):
                    j = hf * 2 + jj
                    P.mm(pb[:, jj * 256:(jj + 1) * 256], uT[:, j, ps["off"] + tt * 128:ps["off"] + (tt + 1) * 128], bdcs, True, True, [f"uT{j}", "cbf"], [pr])
                P.copy(alt(), V[:, tt, hf * 2:hf * 2 + 2, :, :], pb.rearrange("p (j s c) -> p j s c", j=2, s=2), [pr], ["V"])
            if tt % 2 == 1:
                tk()
        CLd, SLd = ps["dft"]
        o, n_ = vo(l, "bf")
        for kb in range(ps["nblk"]):
            sl = slice(kb * NT, (kb + 1) * NT)
            pbs = [(PS[4 + j], f"ps{4 + j}") for j in range(4)] if fixed else [bank() for _ in range(4)]
            for tg in range(nch // TG):
                df, dr_ = dfr.next()
                for s_, M_ in enumerate((CLd, SLd)):
                    P.dma("sp", df[:, s_, :TG, :NT], M_[tg * TG * 128:(tg + 1) * TG * 128, sl].rearrange("(a p) k -> p a k", p=128), [], [dr_])
                for t_ in range(TG):
                    tt = tg * TG + t_
                    for s_ in range(2):
                        for j in range(4):
                            P.mm(pbs[j][0][:, :NT], V[:, tt, j, s_, :], df[:, s_, t_, :NT], tt == 0 and s_ == 0, tt == nch - 1 and s_ == 1, ["V", dr_], [pbs[j][1]])
                tk()
            for j in range(4):
                P.copy(alt(), mixedT[:, j, :NT], pbs[j][0][:, :NT], [pbs[j][1]], ["mixedT"])
            for jo in range(4):
                pb, pr = bank()
                for ji in range(4):
                    P.mm(pb[:, :NT], wf[:, ji, jo * 128:(jo + 1) * 128], mixedT[:, ji, :NT], ji == 0, ji == 3, ["wf", "mixedT"], [pr])
                gl, glr_ = glr.next()
                P.dma("sp", gl[:, :NT], ps["gates"][jo][:, sl], [f"gate{ps['name']}{jo}_{kb}"], [glr_])
                ft, ftr = ftmp.next()
                P.act(ft[:, :NT], pb[:, :NT], AF.Identity, [pr, "vecs"], [ftr], bias=vecs[:, o + jo:o + jo + 1], scale=1.0)
                ms, msr = mst.next()
                P.tt("dve", ms[:, :NT], ft[:, :NT], gl[:, :NT], ALU.mult, [ftr, glr_], [msr])
                P.dma("pool", ps["mix"][jo][:, sl], ms[:, :NT], [msr], [f"mix{ps['name']}{jo}_{kb}"])

    def ssd_common(ps, reset=True):
        if reset:
            sb.top = arena0
        L = ps["L"]; nch = ps["nch"]
        return dict(RL=sb.alloc([128, L], BF16), RS=sb.alloc([128, L], BF16), qend=sb.alloc([128, nch], F32), qtmp=sb.alloc([128, nch], F32),
                    wtok=sb.alloc([128, nch, 48], F32), dtb=sb.alloc([128, nch, 48], F32), Dg=sb.alloc([48, nch, 48], F32))

    def ssd_prep(l, ps, K, roff=0):
        L = ps["L"]; nch = ps["nch"]
        RL = K["RL"]; RS = K["RS"]; qend = K["qend"]; qtmp = K["qtmp"]; wtok = K["wtok"]; dtb_ = K["dtb"]; Dg = K["Dg"]
        rows = hT.rearrange("p c t -> p (c t)").bitcast(F32)
        r_dt = rows[:, roff:roff + L]; r_ln = rows[:, roff + L:roff + 2 * L]; r_a = rows[:, roff + 2 * L:roff + 3 * L]; r_q = rows[:, roff + 3 * L:roff + 4 * L]
        z = zrow[:, ps["off"]:ps["off"] + L]
        c3 = lambda a: a.rearrange("p (c t) -> p c t", t=128)
        R = [f"rows{ps['name']}"]
        P.act(r_dt, z, AF.Ln, ["zrow"], R, bias=1.0, scale=1.0)
        P.act(r_ln, r_dt, AF.Ln, R, R)
        P.ts("dve", r_a, r_dt, Acol[:, 0:1], None, ALU.mult, None, R + ["Acol"], R)
        P.add("dve", lambda e: e.tensor_tensor_scan(r_q, resetm[:, :L], r_a, 0.0, ALU.mult, ALU.add), R + ["resetm"], R)
        P.tt("dve", c3(r_dt), c3(r_a), c3(r_q)[:, :, 127:128].to_broadcast([128, nch, 128]), ALU.add, R, R)
        P.ts("dve", r_dt, r_dt, isb, None, ALU.mult, None, R + ["cf32"], R)
        P.stt(r_q, r_q, sgn, r_dt, ALU.mult, ALU.add, R + ["cf32"], R)
        P.ts("dve", qtmp, c3(r_q)[:, :, 0], isb, None, ALU.mult, None, R + ["cf32"], ["qtmp"])
        P.stt(qend, c3(r_q)[:, :, 127], isf, qtmp, ALU.mult, ALU.add, R + ["cf32", "qtmp"], ["qend"])
        P.copy("dve", RL, r_q, R, ["RL"])
        P.tt("dve", r_dt, r_q, RL, ALU.subtract, R + ["RL"], R)
        P.copy("dve", RL[64:112, :], r_dt[64:112, :], R, ["RL"])
        P.tt("dve", r_a, r_ln, r_q, ALU.subtract, R, R)
        P.copy("dve", RS, r_a, R, ["RS"])
        P.tt("dve", r_dt, r_a, RS, ALU.subtract, R + ["RS"], R)
        P.copy("dve", RS[64:112, :], r_dt[64:112, :], R, ["RS"])
        P.tt("dve", c3(r_dt), c3(r_a), qend.unsqueeze(2).to_broadcast([128, nch, 128]), ALU.add, R + ["qend"], R)
        P.act(r_dt, r_dt, AF.Exp, R, R)
        for c in range(nch):
            pb, pr = bank()
            P.tr(pb[:, 0:48], r_dt[0:48, c * 128:(c + 1) * 128], ident32[0:48, 0:48], R + ["cf32"], [pr])
            P.copy(alt(), wtok[:, c, :], pb[:, 0:48], [pr], ["wtok"])
        P.tt("dve", Dg, ident32[0:48, 0:48].unsqueeze(1).to_broadcast([48, nch, 48]), qend[0:48, :].unsqueeze(2).to_broadcast([48, nch, 48]),
             ALU.mult, ["cf32", "qend"], ["Dg"])
        for c0 in range(0, nch, 8):
            c1 = min(nch, c0 + 8)
            pb, pr = bank()
            P.mm(pb[:, :(c1 - c0) * 48], ones32[0:48, :], Dg[:, c0:c1, :].rearrange("p c q -> p (c q)"), True, True, ["cf32", "Dg"], [pr])
            P.act(dtb_[:, c0:c1, :].rearrange("p c q -> p (c q)"), pb[:, :(c1 - c0) * 48], AF.Exp, [pr], ["dtotbc"])

    def ssd_ctx(ps, K, fwd, scratch=None, gc_n=2, el_n=4, pieces=None):
        C = dict(K)
        C["xsr"] = Rot(sb, "xsc", [128, 1536], BF16, 2); C["btr"] = Rot(sb, "btc", [128, 512], BF16, 2)
        C["xwr"] = Rot(sb, "xw", [128, 24, 64], BF16, 1 if fwd else 2)
        if fwd:
            C["bTr"] = Rot(sb, "bTc", [128, 4, 128], BF16, 2); C["cTr"] = Rot(sb, "cTc", [128, 4, 128], BF16, 2)
            if isinstance(gc_n, int):
                C["gcr"] = Rot(sb, "gc", [128, 1536], BF16, gc_n)
            else:
                C["gcr"] = Rot.__new__(Rot); C["gcr"].b = list(gc_n); C["gcr"].i = 0
            C["cbm"] = [Rot(sb, f"cbm{d}", [128, 4, 128], F32, 1) for d in range(2)]
            C["elr"] = Rot(sb, "el", [128, 3, 128], F32, el_n); C["mtr"] = Rot(sb, "mt", [128, 3, 128], BF16, 16)
            if pieces is not None:
                C["yT"], C["gy"], C["glbr"], C["ywr"] = pieces
            else:
                C["yT"] = scratch[:, 0:3072].bitcast(F32).rearrange("p (j t) -> p j t", t=128)
                C["gy"] = scratch[:, 3072:6144].bitcast(F32).rearrange("p (j t) -> p j t", t=128)
                C["glbr"] = [(scratch[:, 6144 + b_ * 1536:7680 + b_ * 1536].rearrange("p (j t) -> p j t", t=128), f"glb{b_}") for b_ in range(2)]
                C["ywr"] = []
            C["sqb"] = sb.alloc([128, 12, 128], BF16); C["r1"] = sb.alloc([128, 512], F32); C["rstd"] = sb.alloc([128, 512], F32)
            C["msb"] = Rot(sb, "msb", [128, 12, 128], BF16, 1)
        return C

    def load_chunk(ps, C, c, need_ct):
        nm = ps["name"]
        xs_, xr = C["xsr"].next(); bt_, br = C["btr"].next()
        P.dma("sp", xs_, ps["xs"][c * 128:(c + 1) * 128, :], [f"xs{nm}_{c}"], [xr])
        P.dma("sp", bt_, ps["btok"][c * 128:(c + 1) * 128, :], [f"btok{nm}_{c}"], [br])
        res = [xs_, xr, bt_, br]
        if need_ct:
            n = (c * 128) // ps["NT"]
            bT, bTr_ = C["bTr"].next(); cT, cTr_ = C["cTr"].next()
            P.dma("sp", bT, ps["bt"][:, :, c * 128:(c + 1) * 128].rearrange("g p t -> p g t"), [f"bt{nm}{g}_{n}" for g in range(4)], [bTr_])
            P.dma("sp", cT, ps["ct"][:, :, c * 128:(c + 1) * 128].rearrange("g p t -> p g t"), [f"ct{nm}{g}_{n}" for g in range(4)], [cTr_])
            res += [bT, bTr_, cT, cTr_]
        return res

    def make_xw(C, d, c, xs_, xr):
        xw, xwr_ = C["xwr"].next()
        P.tt("pool", xw, xs_.rearrange("p (h e) -> p h e", e=64), C["wtok"][:, c, d * 24:(d + 1) * 24].unsqueeze(2).to_broadcast([128, 24, 64]),
             ALU.mult, [xr, "wtok"], [xwr_])
        return xw, xwr_

    def state_combine(C, d, c, bt_, br, xw, xwr_):
        S = Sst[d]; SR = f"S{d}"
        for g in range(4):
            pb, pr = bank()
            P.mm(pb[:, :384], bt_[:, g * 128:(g + 1) * 128], xw[:, g * 6:(g + 1) * 6, :].rearrange("p h e -> p (h e)"), True, True, [br, xwr_], [pr])
            Sv = S[:, g * 384:(g + 1) * 384]
            P.tt("dve", Sv.rearrange("p (h e) -> p h e", e=64), Sv.rearrange("p (h e) -> p h e", e=64),
                 C["dtb"][:, c, d * 24 + g * 6:d * 24 + g * 6 + 6].unsqueeze(2).to_broadcast([128, 6, 64]), ALU.mult, [SR, "dtotbc"], [SR])
            P.tt("dve", Sv, Sv, pb[:, :384], ALU.add, [SR, pr], [SR])
        P.copy("act", Sbf[d], S, [SR], [f"Sb{d}"])

    def ssd_bwd_steps(ps, C):
        nch = ps["nch"]; nm = ps["name"]
        order = list(range(nch - 1, -1, -1))
        ld = {}; xws = {}

        def step(i):
            c = order[i]
            if i == 0:
                ld[c] = load_chunk(ps, C, c, False)
                xws[c] = make_xw(C, 1, c, ld[c][0], ld[c][1])
            ahead = len(C["xwr"].b) >= 2
            if i + 1 < nch and ahead:
                c1 = order[i + 1]
                ld[c1] = load_chunk(ps, C, c1, False)
                xws[c1] = make_xw(C, 1, c1, ld[c1][0], ld[c1][1])
            P.dma("pool", ps["gst"][c], Sbf[1], ["Sb1"], [f"gst{nm}{c}"])
            state_combine(C, 1, c, ld[c][2], ld[c][3], xws[c][0], xws[c][1])
            del ld[c], xws[c]
            if i + 1 < nch and not ahead:
                c1 = order[i + 1]
                ld[c1] = load_chunk(ps, C, c1, False)
                xws[c1] = make_xw(C, 1, c1, ld[c1][0], ld[c1][1])

        return [lambda i=i: step(i) for i in range(nch)]

    def ssd_fwd(l, ps, C, extra=None, tick=None):
        nch = ps["nch"]; nm = ps["name"]
        RL = C["RL"]; RS = C["RS"]; cbm = C["cbm"]; elr = C["elr"]; mtr = C["mtr"]; gcr = C["gcr"]
        yT = C["yT"]; gy = C["gy"]; glbr = C["glbr"]; sqb = C["sqb"]; r1 = C["r1"]; rstd = C["rstd"]; msb = C["msb"]
        o_snw, n_ = vo(l, "snw")
        extra = list(extra) if extra else []
        tk = tick if tick is not None else (lambda: None)
        ywr = C["ywr"]
        gyb = C.get("gyb")
        CH = {}

        def st_load(c):
            csl = slice(c * 128, (c + 1) * 128)
            xs_, xr, bt_, br, bT, bTr_, cT, cTr_ = load_chunk(ps, C, c, True)
            gc, gcr_ = gcr.next()
            P.dma("sp", gc, ps["gst"][c], [f"gst{nm}{c}"], [gcr_])
            gl_, glr_ = glbr[c % len(glbr)]
            P.dma("sp", gl_, ps["gates"][4:16, :, csl].rearrange("j p t -> p j t"),
                  [f"gate{nm}{j}_{(c * 128) // ps['NT']}" for j in range(4, 16)], [glr_] + ywr)
            CH[c] = dict(csl=csl, xs=xs_, xr=xr, bt=bt_, br=br, bT=bT, bTr=bTr_, cT=cT, cTr=cTr_, gc=gc, gcr=gcr_, gl=gl_, glr=glr_, tiles={}, yb=None)

        def st_A(c):
            k = CH[c]
            pbc, pcr = bank()
            for g in range(4):
                P.mm(pbc[:, g * 128:(g + 1) * 128], k["bT"][:, g, :], k["cT"][:, g, :], True, True, [k["bTr"], k["cTr"]], [pcr])
            k["cm"] = []
            for d in range(2):
                t, tr_ = cbm[d].next()
                P.tt("dve", t, pbc.rearrange("p (g t) -> p g t", g=4), maskd[d].unsqueeze(1).to_broadcast([128, 4, 128]), ALU.mult, [pcr, "cf32"], [tr_])
                k["cm"].append((t, tr_))

        def st_T(c, g):
            k = CH[c]
            tiles = {}
            for d in range(2):
                for half in range(2):
                    for virt in range(2):
                        pb, pr = bank()
                        for i in range(3):
                            h = g * 6 + half * 3 + i
                            q = d * 24 + h
                            o_ = pb[:, i * 128:(i + 1) * 128]
                            P.mm(o_, esel[:, q, :], RL[:, k["csl"]], True, bool(virt), ["cbf", "RL"], [pr])
                            if not virt:
                                P.mm(o_, RS[:, k["csl"]], esel[:, q, :], False, True, ["cbf", "RS"], [pr])
                        el, elr_ = elr.next()
                        P.act(el, pb[:, :384].rearrange("p (i t) -> p i t", i=3), AF.Exp, [pr], [elr_])
                        mt, mtr_ = mtr.next()
                        if virt:
                            P.tt("pool", mt, el, k["cT"][:, g, :].unsqueeze(1).to_broadcast([128, 3, 128]), ALU.mult, [elr_, k["cTr"]], [mtr_])
                        else:
                            cm = k["cm"][d]
                            P.stt(mt, el, BIG, cm[0][:, g, :].unsqueeze(1).to_broadcast([128, 3, 128]), ALU.min, ALU.mult, [elr_, cm[1]], [mtr_])
                        tiles[(d, half, virt)] = (mt, mtr_)
            k["tiles"][g] = tiles

        def st_Y(c, g):
            k = CH[c]
            tiles = k["tiles"][g]
            for jp in range(3):
                j = g * 3 + jp
                if j % 4 == 0:
                    k["yb"] = ybank_()
                pb, pr = k["yb"]
                for hh in range(2):
                    h = 2 * j + hh
                    hl = h - g * 6
                    half, i = hl // 3, hl % 3
                    co = hh * 64
                    o_ = pb[co:co + 64, (j % 4) * 128:(j % 4 + 1) * 128]
                    xh = k["xs"][:, h * 64:(h + 1) * 64]
                    seq = [(xh, tiles[(0, half, 0)], [k["xr"]]), (xh, tiles[(1, half, 0)], [k["xr"]]), (xh, (DI[:, h, :], "DI"), [k["xr"]]),
                           (Sbf[0][:, h * 64:(h + 1) * 64], tiles[(0, half, 1)], ["Sb0"]), (k["gc"][:, h * 64:(h + 1) * 64], tiles[(1, half, 1)], [k["gcr"]])]
                    for si, (lh, (rt, rtr), lr) in enumerate(seq):
                        rhs = rt if rtr == "DI" else rt[:, i, :]
                        P.mm(o_, lh, rhs, si == 0, si == len(seq) - 1, lr + [rtr], [pr], tile_position=(0, co))
                if j % 4 == 3:
                    if gyb is not None:
                        P.tt("dve", gyb[c % 2][:, j - 3:j + 1, :], pb.rearrange("p (k t) -> p k t", k=4), k["gl"][:, j - 3:j + 1, :],
                             ALU.mult, [pr, k["glr"]], [f"gy{c % 2}"])
                    else:
                        P.copy("act", yT[:, j - 3:j + 1, :], pb.rearrange("p (k t) -> p k t", k=4), [pr], ["yT"] + ywr)

        def st_G1(c):
            k = CH[c]
            if gyb is not None:
                P.act(sqb, gyb[c % 2], AF.Square, [f"gy{c % 2}"], ["sqb"])
            else:
                P.tt("pool", gy, yT, k["gl"], ALU.mult, ["yT", k["glr"]], ["gy"])
                P.act(sqb, gy, AF.Square, ["gy"], ["sqb"])

        def st_G2(c):
            pb, pr = bank()
            for g in range(4):
                for i in range(3):
                    P.mm(pb[:, g * 128:(g + 1) * 128], ones, sqb[:, g * 3 + i, :], i == 0, i == 2, ["cbf", "sqb"], [pr])
            P.act(r1, pb, AF.Ln, [pr], ["r1s"], bias=EPS, scale=1.0 / 384.0)
            P.act(rstd, r1, AF.Exp, ["r1s"], ["rstds"], scale=-0.5)

        def st_G3(c, csl):
            ms, msr = msb.next()
            gsrc, gsr = (gyb[c % 2], f"gy{c % 2}") if gyb is not None else (gy, "gy")
            for g in range(4):
                P.tt("dve", ms[:, g * 3:(g + 1) * 3, :], gsrc[:, g * 3:(g + 1) * 3, :], rstd[:, g * 128:(g + 1) * 128].unsqueeze(1).to_broadcast([128, 3, 128]),
                     ALU.mult, [gsr, "rstds"], [msr])
            P.dma("act", ps["mix"][4:16, :, csl].rearrange("j p t -> p j t"), ms, [msr], [f"mixs{nm}_{c}"])

        st_load(0)
        if nch > 1:
            st_load(1)
        st_A(0); st_T(0, 0); st_T(0, 1)
        pend = None
        for c in range(nch):
            k = CH[c]
            xw, xwr_ = make_xw(C, 0, c, k["xs"], k["xr"])
            st_Y(c, 0); tk(); st_T(c, 2); tk(); st_Y(c, 1); tk()
            if pend is not None:
                st_G3(*pend); tk()
            st_T(c, 3); tk(); st_Y(c, 2); tk()
            if c + 1 < nch:
                st_A(c + 1); tk(); st_T(c + 1, 0); tk()
            st_Y(c, 3); tk()
            st_G1(c); tk()
            state_combine(C, 0, c, k["bt"], k["br"], xw, xwr_); tk()
            if c + 1 < nch:
                st_T(c + 1, 1); tk()
            st_G2(c); tk()
            pend = (c, k["csl"])
            if c + 2 < nch:
                st_load(c + 2)
            del CH[c]
            if extra:
                extra.pop(0)()
            if len(extra) > nch - 1 - c and extra:
                extra.pop(0)()
        st_G3(*pend)
        while extra:
            extra.pop(0)()

    def wo_views():
        uflat_ = uT.rearrange("p c t -> p (c t)")
        zflat_ = zrow.bitcast(BF16)
        return [uflat_[:, k * 1024:(k + 1) * 1024] for k in range(9)] + [zflat_[:, k * 1024:(k + 1) * 1024] for k in range(4)]

    def wo_load_step(l, k, ap):
        wv = w_out[l].rearrange("(k p) j -> p k j", p=128)
        o_snw, n_ = vo(l, "snw")
        P.dma("pool", ap, wv[:, k, :], [], [f"wo{k}"])
        if k >= 4:
            P.ts("dve", ap, ap, vecs[:, o_snw + k - 4:o_snw + k - 3], None, ALU.mult, None, [f"wo{k}", "vecs"], [f"wo{k}"])

    def phase_outproj(l, last):
        sb.top = arena0
        fuse = not last
        ln = l + 1
        wo_k = wo_views() + [sb.alloc([128, D], BF16) for _ in range(3)]
        mixr = Rot(sb, "mixb", [128, 16, 512], BF16, 2)
        xor_ = Rot(sb, "xo", [128, 512], F32, 2); xnr = Rot(sb, "xn", [128, 512], F32, 10 if fuse else 2)
        if fuse:
            sqrot = Rot(sb, "sq", [128, 512], BF16, 2); r1 = sb.alloc([128, 512], F32); rstd = sb.alloc([128, 512], F32)
            tmpr = Rot(sb, "ntmp", [128, 512], F32, 2)
            for g in range(3):
                wb, wr = wrot.next()
                P.dma("pool", wb, w_in[ln].rearrange("(c p) j -> p c j", p=128)[:, :, g * 512:(g + 1) * 512], [], [wr])
                winpre[g] = (wb, wr)
        for k in range(13, 16):
            wo_load_step(l, k, wo_k[k])
        blocks = [(ps, n) for ps in ((px,) if last else (pc, px)) for n in range(ps["nblk"])]

        def emit_norm(xns, pn, pnr, NT, pi, hsl):
            P.act(r1[:, :NT], pn[:, :NT], AF.Ln, [pnr], ["r1"], bias=EPS, scale=1.0 / D)
            P.act(rstd[:, :NT], r1[:, :NT], AF.Exp, ["r1"], ["rstd"], scale=-0.5)
            for c in range(8):
                xn, xnr_ = xns[c]
                tm, tmr = tmpr.next()
                P.stt(tm[:, :NT], xn[:, :NT], Amod2[ln % 2][:, c, pi:pi + 1], rstd[:, :NT], ALU.mult, ALU.mult, [xnr_, f"Amod{ln % 2}", "rstd"], [tmr])
                P.act(hT[:, c, hsl], tm[:, :NT], AF.Identity, [tmr, f"modT{ln % 2}"], ["hT"], bias=modT2[ln % 2][:, c, pi:pi + 1], scale=1.0)

        pend_norm = None
        for ps, n in blocks:
            NT = ps["NT"]; nm = ps["name"]; pi = ps["pi"]
            sl = slice(n * NT, (n + 1) * NT)
            hsl = slice(ps["off"] + n * NT, ps["off"] + (n + 1) * NT)
            mb, mbr = mixr.next()
            rd = [f"mix{nm}{j}_{n}" for j in range(4)] + [f"mixs{nm}_{c}" for c in range(n * NT // 128, (n + 1) * NT // 128)]
            P.dma("sp", mb[:, :, :NT], ps["mix"][:, :, sl].rearrange("j p t -> p j t"), rd, [mbr])
            if fuse:
                pn, pnr = ybank_()
            xns = []
            pend_mm = None
            for dc in range(8):
                pb, pr = bank()
                for k in range(16):
                    P.mm(pb[:, :NT], wo_k[k][:, dc * 128:(dc + 1) * 128], mb[:, k, :NT], k == 0, k == 15, [f"wo{k}", mbr], [pr])
                if pend_mm is not None:
                    sq_, sr_, d_ = pend_mm
                    P.mm(pn[:, :NT], ones, sq_[:, :NT], d_ == 0, d_ == 7, [sr_, "cbf"], [pnr])
                    pend_mm = None
                xo, xor__ = xor_.next()
                P.dma("sp", xo[:, :NT], ps["res"][dc][:, sl], [f"res{nm}{dc}"], [xor__])
                xn, xnr_ = xnr.next()
                P.stt(xn[:, :NT], pb[:, :NT], modT2[l % 2][:, 16 + dc, pi:pi + 1], xo[:, :NT], ALU.mult, ALU.add, [pr, f"modT{l % 2}", xor__], [xnr_])
                P.dma("act", ps["res"][dc][:, sl], xn[:, :NT], [xnr_], [f"res{nm}{dc}"])
                if fuse:
                    sq, sr = sqrot.next()
                    P.act(sq[:, :NT], xn[:, :NT], AF.Square, [xnr_], [sr])
                    pend_mm = (sq, sr, dc)
                    xns.append((xn, xnr_))
                    if dc == 1 and pend_norm is not None:
                        emit_norm(*pend_norm)
                        pend_norm = None
            if fuse:
                sq_, sr_, d_ = pend_mm
                P.mm(pn[:, :NT], ones, sq_[:, :NT], d_ == 0, d_ == 7, [sr_, "cbf"], [pnr])
                pend_norm = (xns, pn, pnr, NT, pi, hsl)
        if pend_norm is not None:
            emit_norm(*pend_norm)
        P.barrier()

    def phase_final(ps):
        sb.top = arena0
        NT = ps["NT"]
        xrot = Rot(sb, "xb", [128, 8, NT], F32, 2); sqrot = Rot(sb, "sq", [128, NT], BF16, 2)
        r1 = sb.alloc([128, NT], F32); rstd = sb.alloc([128, NT], F32)
        yb = sb.alloc([128, 8, NT], F32); ost = Rot(sb, "ost", [128, D], F32, 2)
        o = DEPTH * PL
        for n in range(ps["nblk"]):
            sl = slice(n * NT, (n + 1) * NT)
            xb, xr = xrot.next()
            P.dma("sp", xb, ps["res"][:, :, sl].rearrange("c p t -> p c t"), [f"resx{c}" for c in range(8)], [xr])
            rms_block(ps, n, xb, xr, sqrot, r1, rstd, 8, float(D))
            for c in range(8):
                P.stt(yb[:, c, :], xb[:, c, :], vecs[:, o + c:o + c + 1], rstd, ALU.mult, ALU.mult, [xr, "vecs", "rstd"], ["yb"])
            for t in range(NT // 128):
                os_, osr = ost.next()
                for half in range(2):
                    pb, pr = bank()
                    for k in range(4):
                        c = half * 4 + k
                        P.tr(pb[:, k * 128:(k + 1) * 128], yb[:, c, t * 128:(t + 1) * 128], ident32, ["yb", "cf32"], [pr])
                    P.copy(alt(), os_[:, half * 512:(half + 1) * 512], pb, [pr], [osr])
                P.dma("pool", out[n * NT + t * 128:n * NT + (t + 1) * 128, :], os_, [osr], [])

    pc, px = passes["c"], passes["x"]
    load_resid(pc, ctx_in)
    load_resid(px, x_in)
    for st in adaln_steps(0):
        st()
    hflat = hT.rearrange("p c t -> p (c t)")
    uflat = uT.rearrange("p c t -> p (c t)")
    for l in range(depth):
        adaln_small(l)
        for d in range(2):
            P.add("pool", lambda e, d=d: e.memset(Sst[d], 0.0), [], [f"S{d}"])
            P.add("pool", lambda e, d=d: e.memset(Sbf[d], 0.0), [], [f"Sb{d}"])
        if l == 0:
            phase_norm(l)
        phase_inproj(l)
        Kx = ssd_common(px)
        Kc = ssd_common(pc, reset=False)
        ssd_prep(l, pc, Kc, roff=4 * SEQ)
        saved_ops = P.ops
        P.ops = []
        ssd_prep(l, px, Kx, roff=0)
        px_ops = P.ops
        P.ops = saved_ops

        def splice(k=2):
            n_ = 0
            while px_ops and (n_ < k or P.ops[-1][0] == "pe"):
                P.ops.append(px_ops.pop(0))
                n_ += 1

        splice(3)
        zf32 = zrow
        yT_c = zf32[:, 0:1536].rearrange("p (j t) -> p j t", t=128)
        glb_c = zf32[:, 1536:2304].bitcast(BF16).rearrange("p (j t) -> p j t", t=128)
        Cc_gy = None
        wflat = [wrot.b[i_][0].rearrange("p c j -> p (c j)") for i_ in range(2)]
        glb_c2 = wflat[0][:, 0:1536].rearrange("p (j t) -> p j t", t=128)
        gcs = [(wflat[0][:, 1536:3072], "gcc0"), (wflat[1][:, 0:1536], "gcc1")]
        Cc = ssd_ctx(pc, Kc, True, gc_n=gcs, el_n=2, pieces=(yT_c, None, [(glb_c, "glbc0"), (glb_c2, "glbc1")], ["zrow"]))
        Cc["gy"] = sb.alloc([128, 12, 128], F32)
        for st in ssd_bwd_steps(pc, Cc):
            st()
            splice(2)
        ssd_fwd(l, pc, Cc, tick=splice)
        while px_ops:
            splice(4)
        P.barrier()
        Kx = ssd_common(px)
        Cb = ssd_ctx(px, Kx, False)
        F = fft_bufs(l)
        bankpool[0] = [0, 1, 2, 3]
        bsteps = ssd_bwd_steps(px, Cb)

        def tick():
            if bsteps:
                bsteps.pop(0)()

        phase_fft(l, pc, F, tick, fixed=True)
        phase_fft(l, px, F, tick, fixed=True)
        while bsteps:
            bsteps.pop(0)()
        bankpool[0] = list(range(6))
        P.barrier()
        Kx = ssd_common(px)
        Cf = ssd_ctx(px, Kx, True, scratch=hflat[:, 0:9216])
        Cf["gyb"] = [Cf["gy"], hflat[:, 9216:12288].bitcast(F32).rearrange("p (j t) -> p j t", t=128)]
        wov = wo_views()
        ex = (adaln_steps(l + 1) if l + 1 < depth else []) + [(lambda k=k: wo_load_step(l, k, wov[k])) for k in range(13)]
        ssd_fwd(l, px, Cf, extra=ex)
        P.barrier()
        phase_outproj(l, l == depth - 1)
    phase_final(px)
    P.emit()
    nc._sb_peak = sb.peak
    nc._nops = len(P.ops)
    return nc


_CACHE = {}


def _consts():
    if "c" in _CACHE:
        return _CACHE["c"]
    bf = ml_dtypes.bfloat16
    cbf = np.zeros((128, 512 + 48 * 128), np.float32)
    cbf[:, 0:128] = np.eye(128)
    cbf[:, 128:256] = 1.0
    k = np.arange(64)
    ang = 2 * np.pi * np.outer(k, k) / 64.0
    C64, S64 = np.cos(ang), np.sin(ang)
    z = np.zeros((64, 64))
    cbf[:, 256:384] = np.block([[C64, z], [z, C64]])
    cbf[:, 384:512] = np.block([[S64, z], [z, S64]])
    es = np.zeros((128, 48, 128), np.float32)
    for q in range(48):
        es[q, q, :] = 1.0
        es[64 + q, q, :] = 1.0
    cbf[:, 512:] = es.reshape(128, -1)
    cf = np.zeros((128, 516), np.float32)
    cf[:, 0:128] = np.eye(128)
    s = np.arange(128)[:, None]; l_ = np.arange(128)[None, :]
    cf[:, 128:256] = (l_ >= s)
    cf[:, 256:384] = (l_ <= s)
    cf[:, 384:512] = 1.0
    rows = np.arange(128) % 64
    valid = rows < 48
    cf[:, 512] = (valid & (rows < 24))
    cf[:, 513] = (valid & (rows >= 24))
    cf[:, 514] = cf[:, 512] - cf[:, 513]
    dft = {}
    for L in (2048, 256):
        t = np.arange(L, dtype=np.float64)
        a = 2 * np.pi * ((np.outer(t, t)) % L) / L
        sc = 1.0 / np.sqrt(64.0 * L)
        dft[L] = ((np.cos(a) * sc).astype(np.float32).astype(bf), (-np.sin(a) * sc).astype(np.float32).astype(bf))
    _CACHE["c"] = (cbf.astype(bf), cf, dft)
    return _CACHE["c"]


def _pp(v):
    return np.ascontiguousarray(v.reshape(-1, 128).T)


def _vecs(b, c, c_ctx, norm_w, b_ada, conv_w, conv_b, dt_bias, a_log, d_skip, ssd_norm_w, b_fourier, final_norm_w):
    v = np.zeros((128, NV), np.float32)
    for l in range(DEPTH):
        o = l * PL
        v[:, o:o + 8] = _pp(norm_w[l]); v[:, o + 8:o + 32] = _pp(b_ada[l]); v[:, o + 32:o + 52] = _pp(conv_b[l])
        cw = conv_w[l].reshape(9, 2560)
        for t in range(9):
            v[:, o + 52 + t * 20:o + 52 + (t + 1) * 20] = _pp(cw[t])
        for half in (0, 64):
            v[half:half + 48, o + 232] = dt_bias[l].reshape(48)
            v[half:half + 48, o + 233] = a_log[l].reshape(48)
        v[:, o + 234:o + 258] = d_skip[l][None, :]
        v[:, o + 258:o + 270] = _pp(ssd_norm_w[l]); v[:, o + 270:o + 274] = _pp(b_fourier[l])
    o = DEPTH * PL
    v[:, o:o + 8] = _pp(final_norm_w); v[:, o + 8:o + 16] = _pp(c[b]); v[:, o + 16:o + 24] = _pp(c_ctx)
    return v


def make_in_maps(inputs, ncores=8):
    f = lambda a: np.ascontiguousarray(np.asarray(a, dtype=np.float32))
    I = {k: f(v) for k, v in inputs.items()}
    cbf, cf, dft = _consts()
    maps = []
    for b in range(ncores):
        maps.append({
            "x": I["x"][b], "ctx": I["ctx"][b],
            "vecs": _vecs(b, I["c"], I["c_ctx"], I["norm_w"], I["b_ada"], I["conv_w"], I["conv_b"], I["dt_bias"], I["a_log"],
                          I["d_skip"], I["ssd_norm_w"], I["b_fourier"], I["final_norm_w"]),
            "w_ada": I["w_ada"], "w_in": I["w_in"], "w_fourier": I["w_fourier"], "w_out": I["w_out"], "conv_b": I["conv_b"],
            "cbf": cbf, "cf32": cf, "cl2048": dft[2048][0], "sl2048": dft[2048][1], "cl256": dft[256][0], "sl256": dft[256][1],
        })
    return maps


def kernel(**inputs):
    if "nc" not in _CACHE:
        _CACHE["nc"] = build()
    nc = _CACHE["nc"]
    maps = make_in_maps(inputs)
    res = run_bass_kernel_spmd(nc, maps, core_ids=list(range(8)))
    return np.stack([np.asarray(r["out"], dtype=np.float32) for r in res.results], axis=0)
```
